# Optimizing a Trainium2 kernel written in Bass

```python
import jax, jax.numpy as jnp
from jax import lax
import numpy as np

D_MODEL = 1024
BATCH = 4
SEQ = 4096
DEPTH = 4

HEAD_DIM = 64
N_HEADS = D_MODEL // HEAD_DIM
N_HEADS_A = N_HEADS // 2
N_KV_A = N_HEADS_A // 4
N_HEADS_B = N_HEADS - N_HEADS_A
D_FF = 4 * D_MODEL
GRID_W = 64
AXIAL_THETA = 10000.0
ROPE_THETA = 500000.0
ROPE_DIM = HEAD_DIM // 4
BLOCK_Q = 128
DILATED_PATTERNS = ((128, 1), (512, 4), (2048, 16))
NORM_EPS = 1e-6
NEG_INF = -1e30

QA_W = N_HEADS_A * HEAD_DIM
KVA_W = N_KV_A * HEAD_DIM
QB_W = N_HEADS_B * HEAD_DIM
IN_W = QA_W + 2 * KVA_W + 3 * QB_W
MIX_W = QA_W + QB_W

kernel_name = "hybrid_gqa_axial_dilated_encoder"


def rmsnorm(x, g):
    xf = x.astype(jnp.float32)
    y = xf * lax.rsqrt(jnp.mean(xf * xf, axis=-1, keepdims=True) + NORM_EPS)
    return (y * g.astype(jnp.float32)).astype(x.dtype)


def rope(x, pos, theta):
    half = x.shape[-1] // 2
    freqs = theta ** (-jnp.arange(half, dtype=jnp.float32) / half)
    ang = pos.astype(jnp.float32)[:, None] * freqs[None, :]
    cos = jnp.cos(ang)[:, None, :]
    sin = jnp.sin(ang)[:, None, :]
    xf = x.astype(jnp.float32)
    x1, x2 = xf[..., :half], xf[..., half:]
    return jnp.concatenate([x1 * cos - x2 * sin, x2 * cos + x1 * sin], axis=-1).astype(x.dtype)


def axial_rope(x, row_ids, col_ids):
    h = x.shape[-1] // 2
    return jnp.concatenate([rope(x[..., :h], row_ids, AXIAL_THETA),
                            rope(x[..., h:], col_ids, AXIAL_THETA)], axis=-1)


def partial_rope(x, pos):
    return jnp.concatenate([rope(x[..., :ROPE_DIM], pos, ROPE_THETA), x[..., ROPE_DIM:]], axis=-1)


def global_gqa(q, k, v):
    B, S, HA, hd = q.shape
    HKV = k.shape[2]
    G = HA // HKV
    nb = S // BLOCK_Q
    scale = hd ** -0.5
    qb = q.reshape(B, nb, BLOCK_Q, HKV, G, hd).transpose(1, 0, 2, 3, 4, 5)

    def one(qblk):
        s = jnp.einsum('bqkgd,bskd->bkgqs', qblk, k).astype(jnp.float32) * scale
        p = jax.nn.softmax(s, axis=-1).astype(v.dtype)
        return jnp.einsum('bkgqs,bskd->bqkgd', p, v)

    o = lax.map(one, qb)
    return o.transpose(1, 0, 2, 3, 4, 5).reshape(B, S, HA * hd)


def dilated_window_attention(q, k, v):
    B, S, H, hd = q.shape
    nb = S // BLOCK_Q
    scale = hd ** -0.5
    qb = q.reshape(B, nb, BLOCK_Q, H, hd).transpose(1, 0, 2, 3, 4)
    starts = jnp.arange(nb, dtype=jnp.int32) * BLOCK_Q

    def one(args):
        qblk, start = args
        pos = start + jnp.arange(BLOCK_Q, dtype=jnp.int32)
        outs, lses = [], []
        for window, dil in DILATED_PATTERNS:
            n_side = window // (2 * dil)
            offs = jnp.arange(-n_side, n_side + 1, dtype=jnp.int32) * dil
            idx = pos[:, None] + offs[None, :]
            valid = (idx >= 0) & (idx < S)
            idxc = jnp.clip(idx, 0, S - 1)
            kg = k[:, idxc]
            vg = v[:, idxc]
            s = jnp.einsum('bqhd,bqjhd->bhqj', qblk, kg).astype(jnp.float32) * scale
            s = jnp.where(valid[None, None], s, NEG_INF)
            lse = jax.nn.logsumexp(s, axis=-1, keepdims=True)
            p = jnp.exp(s - lse).astype(v.dtype)
            outs.append(jnp.einsum('bhqj,bqjhd->bqhd', p, vg).astype(jnp.float32))
            lses.append(lse[..., 0])
        wts = jax.nn.softmax(jnp.stack(lses, axis=0), axis=0)
        o = jnp.einsum('pbhq,pbqhd->bqhd', wts, jnp.stack(outs, axis=0))
        return o.astype(v.dtype)

    o = lax.map(one, (qb, starts))
    return o.transpose(1, 0, 2, 3, 4).reshape(B, S, H * hd)


def hybrid_layer(x, n1, w_in, q_norm, k_norm, out_norm_a, out_norm_b, w_out,
                 n2, w_mlp_in, w_mlp_out, row_ids, col_ids, pos):
    B, S, _ = x.shape
    h = rmsnorm(x, n1)
    proj = h @ w_in
    cuts = np.cumsum([QA_W, KVA_W, KVA_W, QB_W, QB_W]).tolist()
    qa, ka, va, qb, kb, vb = jnp.split(proj, cuts, axis=-1)

    qa = qa.reshape(B, S, N_HEADS_A, HEAD_DIM)
    ka = ka.reshape(B, S, N_KV_A, HEAD_DIM)
    va = va.reshape(B, S, N_KV_A, HEAD_DIM)
    qa = axial_rope(rmsnorm(qa, q_norm), row_ids, col_ids)
    ka = axial_rope(rmsnorm(ka, k_norm), row_ids, col_ids)
    ya = global_gqa(qa, ka, va)

    qb = partial_rope(qb.reshape(B, S, N_HEADS_B, HEAD_DIM), pos)
    kb = partial_rope(kb.reshape(B, S, N_HEADS_B, HEAD_DIM), pos)
    vb = vb.reshape(B, S, N_HEADS_B, HEAD_DIM)
    yb = dilated_window_attention(qb, kb, vb)

    y = jnp.concatenate([rmsnorm(ya, out_norm_a), rmsnorm(yb, out_norm_b)], axis=-1)
    x = x + y @ w_out

    u = rmsnorm(x, n2) @ w_mlp_in
    u = jnp.square(jax.nn.relu(u))
    return x + u @ w_mlp_out


def setup_inputs(seed: int = 0) -> dict:
    key = jax.random.key(seed)
    ks = jax.random.split(key, 13)
    f32 = jnp.float32

    def gain(k, shape):
        return 1.0 + 0.02 * jax.random.normal(k, shape, f32)

    return {
        "x": jax.random.normal(ks[0], (BATCH, SEQ, D_MODEL), f32),
        "norm1": gain(ks[1], (DEPTH, D_MODEL)),
        "w_in": jax.random.normal(ks[2], (DEPTH, D_MODEL, IN_W), f32) * D_MODEL ** -0.5,
        "q_norm": gain(ks[3], (DEPTH, HEAD_DIM)),
        "k_norm": gain(ks[4], (DEPTH, HEAD_DIM)),
        "out_norm_a": gain(ks[5], (DEPTH, QA_W)),
        "out_norm_b": gain(ks[6], (DEPTH, QB_W)),
        "w_out": jax.random.normal(ks[7], (DEPTH, MIX_W, D_MODEL), f32) * MIX_W ** -0.5,
        "norm2": gain(ks[8], (DEPTH, D_MODEL)),
        "w_mlp_in": jax.random.normal(ks[9], (DEPTH, D_MODEL, D_FF), f32) * D_MODEL ** -0.5,
        "w_mlp_out": jax.random.normal(ks[10], (DEPTH, D_FF, D_MODEL), f32) * D_FF ** -0.5,
        "final_norm": gain(ks[11], (D_MODEL,)),
    }


def reference(x, norm1, w_in, q_norm, k_norm, out_norm_a, out_norm_b, w_out,
              norm2, w_mlp_in, w_mlp_out, final_norm):
    S = x.shape[1]
    rows = S // GRID_W
    row_ids = jnp.repeat(jnp.arange(rows, dtype=jnp.int32), GRID_W)
    col_ids = jnp.tile(jnp.arange(GRID_W, dtype=jnp.int32), rows)
    pos = jnp.arange(S, dtype=jnp.int32)
    for l in range(DEPTH):
        x = hybrid_layer(x, norm1[l], w_in[l], q_norm[l], k_norm[l],
                         out_norm_a[l], out_norm_b[l], w_out[l], norm2[l],
                         w_mlp_in[l], w_mlp_out[l], row_ids, col_ids, pos)
    return rmsnorm(x, final_norm)
```

```python
import numpy as np
from contextlib import ExitStack
import concourse.bass as bass
import concourse.mybir as mybir
from concourse.bass_utils import run_bass_kernel_spmd

F32 = mybir.dt.float32
BF16 = mybir.dt.bfloat16
AF = mybir.ActivationFunctionType
ALU = mybir.AluOpType
AX = mybir.AxisListType

S = 4096
D = 1024
NT = S // 128
HD = 64
DEPTH = 4
IN_W = 2304
DFF = 4096
EPS = 1e-6
SCALE = HD ** -0.5
SELF_SYNC = True
PATTERNS = (1, 4, 16)
NPAIRS = 4
DEBUG_XM = None
DUMP = None
SIMPLE_P4 = False

C_QA, C_KA, C_VA, C_QB, C_KB, C_VB = 0, 512, 640, 768, 1280, 1792


def fap(ap, dims):
    return bass.AP(ap.tensor, ap.offset, [list(ap.ap[0])] + [list(d) for d in dims])


class _Sem:
    __slots__ = ("h", "total", "owner")

    def __init__(self, h, owner=None):
        self.h = h
        self.total = 0
        self.owner = owner


class Buf:
    __slots__ = ("name", "w", "r", "dsem")

    def __init__(self, name):
        self.name = name
        self.w = None
        self.r = {}
        self.dsem = None


class SemPool:
    def __init__(self, es, nc, n):
        self.free = [es.enter_context(nc.semaphore(f"sp{i}")) for i in range(n)]

    def get(self):
        return self.free.pop(0)

    def put(self, hs):
        self.free = list(hs) + self.free


POOL = None
SWSEMS = []


class Phase:
    ENGS = (("pe", "tensor"), ("act", "scalar"), ("dve", "vector"), ("pool", "gpsimd"), ("sp", "sync"))

    def __init__(self, nc, name):
        self.nc = nc
        self.name = name
        self.es = ExitStack()
        self.q = {e: [] for e, _ in self.ENGS}
        self.waited = {e: {} for e, _ in self.ENGS}
        self.esem = {}
        for e in ("pe", "act", "dve", "pool"):
            self.esem[e] = _Sem(POOL.get(), self)
        self.dsems = []
        self.swsems = []
        self.nsw = 0

    def _dma_sem(self, buf, sw=False):
        if buf.dsem is None or buf.dsem[0] is not self:
            if sw:
                s = SWSEMS[self.nsw]
                self.nsw += 1
                s.owner = self
                self.swsems.append(s)
            else:
                s = _Sem(POOL.get(), self)
                self.dsems.append(s)
            buf.dsem = (self, s)
        return buf.dsem[1]

    def op(self, eng, fn, reads=(), writes=(), dma=None, after=()):
        deps = []
        for b in after:
            if b.w is not None:
                deps.append(b.w)
            deps.extend(b.r.values())
        for b in reads:
            if b.w is not None:
                deps.append(b.w)
        for b in writes:
            if b.w is not None:
                deps.append(b.w)
            deps.extend(b.r.values())
        waits = {}
        for (s, v, src, phs) in deps:
            if phs is not self:
                continue
            if src == eng and (eng == "pe" or not SELF_SYNC):
                continue
            if self.waited[eng].get(s, 0) >= v:
                continue
            if src == "dma":
                assert v == s.total, f"DMA semaphore reuse hazard in {self.name}"
            if waits.get(s, 0) < v:
                waits[s] = v
        for s, v in waits.items():
            self.waited[eng][s] = v
        if dma is not None:
            sem = self._dma_sem(dma, sw=(eng == "pool"))
            sem.total += 16
            ev = (sem, sem.total, "dma", self)
            amt = 16
        else:
            sem = self.esem[eng]
            sem.total += 1
            ev = (sem, sem.total, eng, self)
            amt = 1
        self.q[eng].append((list(waits.items()), fn, sem, amt))
        for b in reads:
            old = b.r.get(ev[0])
            if old is None or old[3] is not self or old[1] < ev[1]:
                b.r[ev[0]] = ev
        for b in writes:
            b.w = ev
            b.r = {}
        return ev

    def close(self):
        fin = [(s, s.total) for s in self.dsems + self.swsems if s.total > 0 and self.waited["sp"].get(s, 0) < s.total]
        self.q["sp"].append((fin, None, None, 0))
        allsems = list(self.esem.values()) + self.dsems
        with self.nc.Block() as cb:
            def fclear(e):
                for s in allsems:
                    e.sem_clear(s.h)
            cb.gpsimd(fclear)
        with self.nc.Block() as block:
            for eng, attr in self.ENGS:
                items = self.q[eng]
                if not items:
                    continue

                def f(e, items=items):
                    for waits, fn, sem, amt in items:
                        for (s, v) in waits:
                            e.wait_ge(s.h, v)
                        if fn is not None:
                            fn(e).then_inc(sem.h, amt)
                getattr(block, attr)(f)
        POOL.put([s.h for s in allsems])
        self.es.close()


class Tile:
    def __init__(self, es, nc, name, shape, dtype, psum=False):
        if psum:
            self.t = es.enter_context(nc.psum_tensor(name, shape, dtype))
        else:
            self.t = es.enter_context(nc.sbuf_tensor(name, shape, dtype))
        self.b = Buf(name)

        self.subs = {}

    def sub(self, key):
        if key not in self.subs:
            self.subs[key] = Buf(f"{self.b.name}_{key}")
        return self.subs[key]

    def __getitem__(self, k):
        return self.t[k]


def build_program(n_layers=DEPTH, debug=False, nt1=NT, phases=(1, 2, 3, 4)):
    nc = bass.Bass("TRN2", target_bir_lowering=False)
    dk = "ExternalOutput" if debug else "Internal"
    x_in = nc.dram_tensor("x", [S, D], F32, kind="ExternalInput").ap()
    w_in = nc.dram_tensor("w_in", [DEPTH, D, IN_W], F32, kind="ExternalInput").ap()
    w_out = nc.dram_tensor("w_out", [DEPTH, D, D], F32, kind="ExternalInput").ap()
    w_mi = nc.dram_tensor("w_mlp_in", [DEPTH, D, DFF], F32, kind="ExternalInput").ap()
    w_mo = nc.dram_tensor("w_mlp_out", [DEPTH, DFF, D], F32, kind="ExternalInput").ap()
    g_n1 = nc.dram_tensor("norm1", [DEPTH, D], F32, kind="ExternalInput").ap()
    g_n2 = nc.dram_tensor("norm2", [DEPTH, D], F32, kind="ExternalInput").ap()
    g_q = nc.dram_tensor("q_norm", [DEPTH, HD], F32, kind="ExternalInput").ap()
    g_k = nc.dram_tensor("k_norm", [DEPTH, HD], F32, kind="ExternalInput").ap()
    g_oa = nc.dram_tensor("out_norm_a", [DEPTH, 512], F32, kind="ExternalInput").ap()
    g_ob = nc.dram_tensor("out_norm_b", [DEPTH, 512], F32, kind="ExternalInput").ap()
    g_fin = nc.dram_tensor("final_norm", [1, D], F32, kind="ExternalInput").ap()
    c_ident = nc.dram_tensor("c_ident", [128, 128], F32, kind="ExternalInput").ap()
    c_ropeA = nc.dram_tensor("c_ropeA", [S, 128], F32, kind="ExternalInput").ap()
    c_ropeB = nc.dram_tensor("c_ropeB", [S, 32], F32, kind="ExternalInput").ap()
    c_mask = nc.dram_tensor("c_mask", [128, 384], F32, kind="ExternalInput").ap()
    out = nc.dram_tensor("out", [S, D], F32, kind="ExternalOutput").ap()
    proj = nc.dram_tensor("proj", [S, IN_W], BF16, kind=dk).ap()
    ybuf = nc.dram_tensor("ybuf", [S, D], BF16, kind=dk).ap()
    xres = nc.dram_tensor("xres", [S, D], F32, kind=dk).ap()

    global DEBUG_XM
    DEBUG_XM = xres if (debug and n_layers == 1) else None
    global POOL, SWSEMS
    with ExitStack() as ges:
        POOL = SemPool(ges, nc, 64)
        SWSEMS[:] = [_Sem(ges.enter_context(nc.semaphore(f"sw{i}"))) for i in range(24)]
        ident_f = Tile(ges, nc, "ident_f", [128, 128], F32)
        ident_b = Tile(ges, nc, "ident_b", [128, 128], BF16)
        wi_g = [Tile(ges, nc, f"wi_g{k}", [128, DFF], BF16) for k in range(8)]
        ph = Phase(nc, "c0")
        ph.op("sp", lambda e: e.dma_start(out=ident_f[:], in_=c_ident[:, :]), writes=[ident_f.b], dma=ident_f.b)
        ph.op("pool", lambda e: e.dma_start(out=ident_b[:], in_=c_ident[:, :]), writes=[ident_b.b], dma=ident_b.b)
        ph.close()

        for l in range(n_layers):
            x_src = x_in if l == 0 else xres
            if 1 in phases:
                phase1(nc, l, x_src, w_in, g_n1, g_q, g_k, c_ropeA, c_ropeB, proj, ident_b, nt1)
            if 2 in phases:
                phase2(nc, l, proj, ybuf, ident_b, ident_f, wi_g if 4 in phases else None, w_mi)
            if 3 in phases:
                phase3(nc, l, proj, ybuf, c_mask, ident_b, ident_f)
            if 4 in phases:
                last = (l == n_layers - 1)
                phase4(nc, l, x_src, ybuf, w_out, w_mi, w_mo, g_oa, g_ob, g_n2, g_fin,
                       out if last else xres, last, ident_b, nt1, wi_g if 2 in phases else None)
    return nc


def rstd_ops(ph, ss, tmp, rs, scale, n=None):
    sl = slice(None) if n is None else slice(0, n)
    ph.op("dve", lambda e: e.tensor_scalar(out=tmp[:, sl], in0=ss[:, sl], scalar1=scale, scalar2=EPS, op0=ALU.mult, op1=ALU.add),
          reads=[ss.b], writes=[tmp.b])
    ph.op("act", lambda e: e.activation(out=tmp[:, sl], in_=tmp[:, sl], func=AF.Sqrt), reads=[tmp.b], writes=[tmp.b])
    ph.op("dve", lambda e: e.reciprocal(out=rs[:, sl], in_=tmp[:, sl]), reads=[tmp.b], writes=[rs.b])


def transpose8(ph, src, dst, pT, ident_b, evac_eng):
    for kc in range(8):
        ph.op("pe", lambda e, kc=kc: e.transpose(out=pT[:, kc, :], in_=src[:, kc * 128:(kc + 1) * 128], identity=ident_b[:]),
              reads=[src.b, ident_b.b], writes=[pT.b])
    if evac_eng == "act":
        ph.op("act", lambda e: e.copy(out=dst[:], in_=pT[:]), reads=[pT.b], writes=[dst.b])
    else:
        ph.op("dve", lambda e: e.tensor_copy(out=dst[:], in_=pT[:]), reads=[pT.b], writes=[dst.b])


def phase1(nc, l, x_src, w_in, g_n1, g_q, g_k, c_ropeA, c_ropeB, proj, ident_b, nt):
    ph = Phase(nc, f"p1l{l}")
    with ExitStack() as es:
        T = lambda name, shape, dt, psum=False: Tile(es, nc, f"p1l{l}_{name}", shape, dt, psum)
        wk = [T(f"w{kc}", [128, IN_W], BF16) for kc in range(8)]
        g1 = T("g1", [128, D], F32)
        gq = T("gq", [128, HD], F32)
        gk = T("gk", [128, HD], F32)
        ropeA = T("ropeA", [128, NT, 128], F32)
        ropeB = T("ropeB", [128, NT, 32], F32)
        NB = 2
        x_t = [T(f"x{i}", [128, D], F32) for i in range(3)]
        junk = [T(f"junk{i}", [128, D], BF16) for i in range(NB)]
        ss = [T(f"ss{i}", [128, 1], F32) for i in range(NB)]
        sd = [T(f"sd{i}", [128, 1], F32) for i in range(NB)]
        rs = [T(f"rs{i}", [128, 1], F32) for i in range(NB)]
        h_b = [T(f"h{i}", [128, D], BF16) for i in range(NB)]
        hT = [T(f"hT{i}", [128, 8, 128], BF16) for i in range(NB)]
        po = [T(f"po{i}", [128, IN_W], BF16) for i in range(NB)]
        sq = [T(f"sq{i}", [128, 640], F32) for i in range(NB)]
        s8 = [T(f"s8{i}", [128, 10], F32) for i in range(NB)]
        d8 = [T(f"d8{i}", [128, 10], F32) for i in range(NB)]
        r8 = [T(f"r8{i}", [128, 10], F32) for i in range(NB)]
        qn = [T(f"qn{i}", [128, 640], F32) for i in range(NB)]
        t1 = [T(f"t1{i}", [128, 640], F32) for i in range(NB)]
        t2 = [T(f"t2{i}", [128, 640], F32) for i in range(NB)]
        u1 = [T(f"u1{i}", [128, 256], F32) for i in range(NB)]
        u2 = [T(f"u2{i}", [128, 256], F32) for i in range(NB)]
        f0 = [T(f"f0{i}", [128, 640], F32) for i in range(NB)]
        f1 = f0
        fb = [T(f"fb{i}", [128, 256], F32) for i in range(NB)]
        pT = [T(f"pT{i}", [128, 8, 128], BF16, psum=True) for i in range(1)]
        pp = [T(f"pp{i}", [128, 512], F32, psum=True) for i in range(5)]

        wv = w_in[l].rearrange("(kc p) n -> kc p n", p=128)
        for kc in range(8):
            ph.op("pool", lambda e, kc=kc: e.dma_start(out=wk[kc][:], in_=wv[kc]), writes=[wk[kc].b], dma=wk[kc].b)
        ph.op("sp", lambda e: e.dma_start(out=g1[:], in_=g_n1[l:l + 1, :].partition_broadcast(128)), writes=[g1.b], dma=g1.b)
        ph.op("sp", lambda e: e.dma_start(out=gq[:], in_=g_q[l:l + 1, :].partition_broadcast(128)), writes=[gq.b], dma=gq.b)
        ph.op("sp", lambda e: e.dma_start(out=gk[:], in_=g_k[l:l + 1, :].partition_broadcast(128)), writes=[gk.b], dma=gk.b)
        ph.op("sp", lambda e: e.dma_start(out=ropeA[:], in_=c_ropeA.rearrange("(t p) c -> p t c", p=128)), writes=[ropeA.b], dma=ropeA.b)
        ph.op("sp", lambda e: e.dma_start(out=ropeB[:], in_=c_ropeB.rearrange("(t p) c -> p t c", p=128)), writes=[ropeB.b], dma=ropeB.b)

        groups = [(C_QA, 512), (C_KA, 256), (C_QB, 512), (C_KB, 512), (C_VB, 512)]

        def slot(t):
            i = t % NB
            return (x_t[t % 3], junk[i], ss[i], sd[i], rs[i], h_b[i], hT[i], po[i], sq[i], s8[i], d8[i], r8[i], qn[i], t1[i], t2[i], u1[i], u2[i])

        def load_x(t):
            X = x_t[t % 3]
            ph.op("sp", lambda e, X=X, t=t: e.dma_start(out=X[:], in_=x_src[t * 128:(t + 1) * 128, :]), writes=[X.b], dma=X.b)

        def stage_N(t):
            X, J, SS, SD, RS, H, HT, PO, SQ, S8, D8, R8, QN, T1, T2, U1, U2 = slot(t)
            ph.op("act", lambda e, X=X, J=J, SS=SS: e.activation(out=J[:], in_=X[:], func=AF.Square, accum_out=SS[:]),
                  reads=[X.b], writes=[J.b, SS.b])
            rstd_ops(ph, SS, SD, RS, 1.0 / D)
            ph.op("dve", lambda e, X=X, RS=RS, H=H: e.scalar_tensor_tensor(out=H[:], in0=X[:], scalar=RS[:, 0:1], in1=g1[:],
                                                                         op0=ALU.mult, op1=ALU.mult),
                  reads=[X.b, RS.b, g1.b], writes=[H.b])

        def stage_T(t):
            X, J, SS, SD, RS, H, HT, PO, SQ, S8, D8, R8, QN, T1, T2, U1, U2 = slot(t)
            transpose8(ph, H, HT, pT[0], ident_b, "dve")

        def stage_M(t):
            X, J, SS, SD, RS, H, HT, PO, SQ, S8, D8, R8, QN, T1, T2, U1, U2 = slot(t)
            for gi, (c0, wd) in enumerate(groups):
                for kc in range(8):
                    ph.op("pe", lambda e, gi=gi, c0=c0, wd=wd, kc=kc, HT=HT: e.matmul(
                        pp[gi][:, 0:wd], lhsT=HT[:, kc, :], rhs=wk[kc][:, c0:c0 + wd], start=(kc == 0), stop=(kc == 7)),
                        reads=[HT.b, wk[kc].b], writes=[pp[gi].b])

        def stage_Pe_a(t):
            X, J, SS, SD, RS, H, HT, PO, SQ, S8, D8, R8, QN, T1, T2, U1, U2 = slot(t)
            i = t % NB
            F0, F1, FB = f0[i], f1[i], fb[i]
            ph.op("act", lambda e: e.copy(out=F0[:, 0:512], in_=pp[0][:, 0:512]), reads=[pp[0].b], writes=[F0.b])
            ph.op("dve", lambda e: e.tensor_copy(out=F0[:, 512:640], in_=pp[1][:, 0:128]), reads=[pp[1].b], writes=[F1.b])
            ph.op("dve", lambda e: e.tensor_copy(out=PO[:, C_VA:C_VA + 128], in_=pp[1][:, 128:256]), reads=[pp[1].b], writes=[PO.sub('va')])
            ph.op("act", lambda e: e.copy(out=PO[:, C_QB:C_QB + 512], in_=pp[2][:, :]), reads=[pp[2].b], writes=[PO.sub(2)])
            ph.op("act", lambda e: e.copy(out=PO[:, C_KB:C_KB + 512], in_=pp[3][:, :]), reads=[pp[3].b], writes=[PO.sub(3)])
            ph.op("act", lambda e: e.copy(out=PO[:, C_VB:C_VB + 512], in_=pp[4][:, :]), reads=[pp[4].b], writes=[PO.sub('vb')])

        def stage_Pe_b(t):
            X, J, SS, SD, RS, H, HT, PO, SQ, S8, D8, R8, QN, T1, T2, U1, U2 = slot(t)
            i = t % NB
            F0, F1, FB = f0[i], f1[i], fb[i]
            ph.op("dve", lambda e: e.tensor_copy(out=FB[:, 0:128].rearrange("p (h d) -> p h d", d=16), in_=fap(pp[2][:, 0:16], [(64, 8), (1, 16)])),
                  reads=[pp[2].b, PO.sub(2)], writes=[FB.sub(2)])
            ph.op("dve", lambda e: e.tensor_copy(out=FB[:, 128:256].rearrange("p (h d) -> p h d", d=16), in_=fap(pp[3][:, 0:16], [(64, 8), (1, 16)])),
                  reads=[pp[3].b, PO.sub(3)], writes=[FB.sub(3)])

        def stage_Pm(t):
            X, J, SS, SD, RS, H, HT, PO, SQ, S8, D8, R8, QN, T1, T2, U1, U2 = slot(t)
            i = t % NB
            F0, F1, FB = f0[i], f1[i], fb[i]
            ph.op("act", lambda e: e.activation(out=SQ[:, 0:640], in_=F0[:, 0:640], func=AF.Square),
                  reads=[F0.b, F1.b], writes=[SQ.b])
            ph.op("dve", lambda e: e.tensor_reduce(out=S8[:], in_=SQ[:].rearrange("p (h d) -> p h d", d=HD), axis=AX.X, op=ALU.add),
                  reads=[SQ.b], writes=[S8.b])
            rstd_ops(ph, S8, D8, R8, 1.0 / HD)
            ph.op("dve", lambda e: e.tensor_tensor(out=QN[:, 0:640].rearrange("p (h d) -> p h d", d=HD),
                                                   in0=F0[:, 0:640].rearrange("p (h d) -> p h d", d=HD),
                                                   in1=fap(R8[:, 0:10], [(1, 10), (0, HD)]), op=ALU.mult),
                  reads=[F0.b, F1.b, R8.b], writes=[QN.b])
            ph.op("dve", lambda e: e.tensor_tensor(out=QN[:, 0:512].rearrange("p (h d) -> p h d", d=HD),
                                                    in0=QN[:, 0:512].rearrange("p (h d) -> p h d", d=HD),
                                                    in1=fap(gq[:], [(0, 8), (1, HD)]), op=ALU.mult),
                  reads=[QN.b, gq.b], writes=[QN.b])
            ph.op("dve", lambda e: e.tensor_tensor(out=QN[:, 512:640].rearrange("p (h d) -> p h d", d=HD),
                                                    in0=QN[:, 512:640].rearrange("p (h d) -> p h d", d=HD),
                                                    in1=fap(gk[:], [(0, 2), (1, HD)]), op=ALU.mult),
                  reads=[QN.b, gk.b], writes=[QN.b])
            cosA = fap(ropeA[:, t, 0:64], [(0, 10), (1, 64)])
            ph.op("dve", lambda e: e.tensor_tensor(out=T1[:].rearrange("p (h d) -> p h d", d=HD),
                                                   in0=QN[:].rearrange("p (h d) -> p h d", d=HD), in1=cosA, op=ALU.mult),
                  reads=[QN.b, ropeA.b], writes=[T1.b])
            for half in range(2):
                sn = fap(ropeA[:, t, 64 + half * 16:64 + half * 16 + 16], [(0, 10), (32, 2), (1, 16)])
                dst = fap(T2[:, half * 16:half * 16 + 16], [(64, 10), (32, 2), (1, 16)])
                src = fap(QN[:, (1 - half) * 16:(1 - half) * 16 + 16], [(64, 10), (32, 2), (1, 16)])
                ph.op("pool", lambda e, dst=dst, src=src, sn=sn: e.tensor_tensor(out=dst, in0=src, in1=sn, op=ALU.mult),
                      reads=[QN.b, ropeA.b], writes=[T2.b])
            ph.op("dve", lambda e: e.tensor_tensor(out=PO[:, C_QA:C_QA + 640], in0=T1[:], in1=T2[:], op=ALU.add),
                  reads=[T1.b, T2.b], writes=[PO.sub('qa')])
            for gi, c0, off in ((2, C_QB, 0), (3, C_KB, 128)):
                xin = FB[:, off:off + 128].rearrange("p (h d) -> p h d", d=16)
                cosB = fap(ropeB[:, t, 0:16], [(0, 8), (1, 16)])
                ph.op("pool", lambda e, xin=xin, cosB=cosB, off=off: e.tensor_tensor(
                    out=U1[:, off:off + 128].rearrange("p (h d) -> p h d", d=16), in0=xin, in1=cosB, op=ALU.mult),
                    reads=[FB.sub(gi), ropeB.b], writes=[U1.sub(gi)])
                for half in range(2):
                    xs = fap(FB[:, off + (1 - half) * 8:off + (1 - half) * 8 + 8], [(16, 8), (1, 8)])
                    sn = fap(ropeB[:, t, 16 + half * 8:16 + half * 8 + 8], [(0, 8), (1, 8)])
                    dst = fap(U2[:, off + half * 8:off + half * 8 + 8], [(16, 8), (1, 8)])
                    ph.op("pool", lambda e, dst=dst, xs=xs, sn=sn: e.tensor_tensor(out=dst, in0=xs, in1=sn, op=ALU.mult),
                          reads=[FB.sub(gi), ropeB.b], writes=[U2.sub(gi)])
                ph.op("dve", lambda e, c0=c0, off=off: e.tensor_tensor(
                    out=fap(PO[:, c0:c0 + 16], [(64, 8), (1, 16)]),
                    in0=U1[:, off:off + 128].rearrange("p (h d) -> p h d", d=16),
                    in1=U2[:, off:off + 128].rearrange("p (h d) -> p h d", d=16), op=ALU.add),
                    reads=[U1.sub(gi), U2.sub(gi)], writes=[PO.sub(gi)])
            ph.op("sp", lambda e: e.dma_start(out=proj[t * 128:(t + 1) * 128, :], in_=PO[:]),
                  reads=[PO.sub('qa'), PO.sub('va'), PO.sub(2), PO.sub(3), PO.sub('vb')], dma=PO.b)

        for t0 in range(min(3, nt)):
            load_x(t0)
        stage_N(0)
        stage_T(0)
        stage_M(0)
        if nt > 1:
            stage_N(1)
        for k in range(nt):
            if k + 3 < nt:
                load_x(k + 3)
            stage_Pe_a(k)
            if k + 1 < nt:
                stage_T(k + 1)
            stage_Pe_b(k)
            if k + 1 < nt:
                stage_M(k + 1)
            if k + 2 < nt:
                stage_N(k + 2)
            stage_Pm(k)
        ph.close()


def load_transposed(ph, proj, c0, tm, dstT, ptr, ident_b, row_sel=None):
    pv = proj.rearrange("(t p) c -> p t c", p=128)
    for q4 in range(4):
        ph.op("sp", lambda e, q4=q4: e.dma_start(out=tm[:, q4 * 8:(q4 + 1) * 8, :], in_=pv[:, q4 * 8:(q4 + 1) * 8, c0:c0 + 128]),
              writes=[tm.sub(q4)], dma=tm.sub(q4))
    for q4 in range(4):
        for t in range(q4 * 8, q4 * 8 + 8):
            ph.op("pe", lambda e, t=t: e.transpose(out=ptr[:, t % 8, :], in_=tm[:, t, :], identity=ident_b[:]),
                  reads=[tm.sub(q4), ident_b.b], writes=[ptr.b])
        ph.op("dve", lambda e, q4=q4: e.tensor_copy(out=dstT[:, q4 * 1024:(q4 + 1) * 1024], in_=ptr[:].rearrange("p a b -> p (a b)")),
              reads=[ptr.b], writes=[dstT.b])


def phase2(nc, l, proj, ybuf, ident_b, ident_f, wi_g, w_mi):
    ph = Phase(nc, f"p2l{l}")
    with ExitStack() as es:
        T = lambda name, shape, dt, psum=False: Tile(es, nc, f"p2l{l}_{name}", shape, dt, psum)
        kT = T("kT", [128, S], BF16)
        vext = T("vext", [128, NT * 2 * 65], BF16)
        qT = [T(f"qT{j}", [128, S], BF16) for j in range(4)]
        tm = [T(f"tm{i}", [128, NT, 128], BF16) for i in range(2)]
        pb = [T(f"pb{i}", [128, 1024], BF16) for i in range(3)]
        oT = [T(f"oT{i}", [128, 1024], F32) for i in range(2)]
        rc = [T(f"rc{i}", [128, 4], F32) for i in range(2)]
        yst = [T(f"yst{i}", [128, 4, 512], BF16) for i in range(2)]
        ptr = T("ptr", [128, 8, 128], BF16, psum=True)
        ps = [T(f"ps{i}", [128, 1024], F32, psum=True) for i in range(2)]
        pov = T("pov", [128, 1024], F32, psum=True)
        pot = T("pot", [128, 4, 128], F32, psum=True)

        v4 = vext.t[:].rearrange("p (t g c) -> p t g c", t=NT, g=2, c=65)
        ph.op("pool", lambda e: e.memset(v4[:, :, :, 64:65], 1.0), writes=[vext.sub("ones")])
        if wi_g is not None:
            wiv = w_mi[l].rearrange("(kc p) n -> kc p n", p=128)
            for k in range(8):
                ph.op("pool", lambda e, k=k: e.dma_start(out=wi_g[k][:], in_=wiv[k]), writes=[wi_g[k].b], dma=wi_g[k].b)
        pv = proj.rearrange("(t p) c -> p t c", p=128)
        for q4 in range(4):
            for g in range(2):
                ph.op("sp", lambda e, q4=q4, g=g: e.dma_start(
                    out=v4[:, q4 * 8:(q4 + 1) * 8, g, 0:64],
                    in_=pv[:, q4 * 8:(q4 + 1) * 8, C_VA + g * 64:C_VA + g * 64 + 64]),
                    writes=[vext.sub((q4, g))], dma=vext.sub((q4, g)))
        load_transposed(ph, proj, C_KA, tm[0], kT, ptr, ident_b)
        for j in range(4):
            load_transposed(ph, proj, C_QA + j * 128, tm[(j + 1) % 2], qT[j], ptr, ident_b)

        steps = [(qc, j, kt) for qc in range(S // 512) for j in range(4) for kt in range(NT)]
        N = len(steps)
        deferred = {}

        def finalize_pe(grp, qc, j):
            a = grp % 2
            Y = yst[qc % 2]
            for g in range(2):
                for c in range(4):
                    ph.op("pe", lambda e, c=c, g=g: e.transpose(out=pot[:, c, 0:65],
                                                              in_=oT[a][0:65, g * 512 + c * 128:g * 512 + (c + 1) * 128],
                                                              identity=ident_f[0:65, 0:65]),
                          reads=[oT[a].b, ident_f.b], writes=[pot.b])
                ph.op("dve", lambda e, g=g: e.reciprocal(out=rc[g][:], in_=pot[:, :, 64]), reads=[pot.b], writes=[rc[g].b])
                hc = (2 * j + g) * 64
                ph.op("dve", lambda e, g=g, hc=hc: e.tensor_tensor(out=Y[:, :, hc:hc + 64], in0=pot[:, :, 0:64],
                                                                   in1=fap(rc[g][:], [(1, 4), (0, 64)]), op=ALU.mult),
                      reads=[pot.b, rc[g].b], writes=[Y.sub(hc)])
            if j == 3:
                ph.op("sp", lambda e: e.dma_start(
                    out=ybuf[qc * 512:(qc + 1) * 512, 0:512].rearrange("(c p) f -> p c f", p=128), in_=Y[:, :, :]),
                    reads=[Y.sub(h * 64) for h in range(8)], dma=Y.b)

        for n in range(N + 4):
            if n < N:
                qc, j, kt = steps[n]
                for g in range(2):
                    r0 = g * 64
                    ph.op("pe", lambda e, n=n, j=j, r0=r0, kt=kt, qc=qc, g=g: e.matmul(
                        ps[n % 2][:, g * 512:(g + 1) * 512], lhsT=kT[r0:r0 + 64, kt * 128:(kt + 1) * 128],
                        rhs=qT[j][r0:r0 + 64, qc * 512:(qc + 1) * 512], start=True, stop=True),
                        reads=[kT.b, qT[j].b], writes=[ps[n % 2].b])
                ph.op("act", lambda e, n=n: e.activation(out=pb[n % 3][:], in_=ps[n % 2][:], func=AF.Exp, scale=SCALE),
                      reads=[ps[n % 2].b], writes=[pb[n % 3].b])
            m = n - 1
            if 0 <= m < N:
                qc, j, kt = steps[m]
                grp = m // NT
                a = grp % 2
                for g in range(2):
                    ph.op("pe", lambda e, m=m, g=g, kt=kt: e.matmul(
                        pov[0:65, g * 512:(g + 1) * 512], lhsT=v4[:, kt, g, :], rhs=pb[m % 3][:, g * 512:(g + 1) * 512],
                        start=(kt == 0), stop=(kt == NT - 1)),
                        reads=[pb[m % 3].b, vext.sub((kt // 8, g)), vext.sub("ones")], writes=[pov.b])
                if kt == NT - 1:
                    ph.op("dve", lambda e, a=a: e.tensor_copy(out=oT[a][0:65, :], in_=pov[0:65, :]),
                          reads=[pov.b], writes=[oT[a].b])
                    deferred.setdefault(n + 3, []).append((grp, qc, j))
            for args in deferred.pop(n, []):
                finalize_pe(*args)
        assert not deferred
        ph.close()


def dram_ap(base, row0, c0, dims):
    return bass.AP(base.tensor, base.offset + row0 * IN_W + c0, [list(d) for d in dims])


def pi_dma_specs(d):
    nb = NT // d
    specs = []
    if d == 1:
        for q4 in range(4):
            specs.append(((q4 * 8, 8, 1), q4 * 8 * 128, (1, 128)))
    elif d == 4:
        for r in range(4):
            specs.append(((r * nb, nb, 1), r, (d, 128 * d)))
    else:
        for ib in range(nb):
            specs.append(((ib, d, nb), d * 128 * ib, (d, 1)))
    return specs


def phase3(nc, l, proj, ybuf, c_mask, ident_b, ident_f):
    ph = Phase(nc, f"p3l{l}")
    with ExitStack() as es:
        T = lambda name, shape, dt, psum=False: Tile(es, nc, f"p3l{l}_{name}", shape, dt, psum)
        mask_f = T("mask_f", [128, 384], F32)
        mask = T("mask", [128, 384], BF16)
        qtm = [T(f"qtm{i}", [128, NT, 128], BF16) for i in range(2)]
        ktm = [T(f"ktm{i}", [128, NT, 128], BF16) for i in range(2)]
        qT = [T(f"qT{i}", [128, S], BF16) for i in range(2)]
        kT = [T(f"kT{i}", [128, S], BF16) for i in range(2)]
        vext = [T(f"vext{i}", [128, NT * 2 * 65], BF16) for i in range(2)]
        accT = [T(f"acc{i}", [128, S], F32) for i in range(2)]
        pb = [T(f"pb{i}", [128, 384], BF16) for i in range(3)]
        pbm = [T(f"pbm{i}", [128, 384], BF16) for i in range(3)]
        rc = [T(f"rc{i}", [128, 4], F32) for i in range(2)]
        yst = [T(f"yst{i}", [128, 4, 128], BF16) for i in range(2)]
        ptr = T("ptr", [128, 8, 128], BF16, psum=True)
        ptr2 = T("ptr2", [128, 8, 128], BF16, psum=True)
        potc = [T(f"potc{i}", [128, 4, 65], F32) for i in range(2)]
        ps = [T(f"ps{i}", [128, 512], F32, psum=True) for i in range(3)]
        pov = [T(f"pov{i}", [128, 512], F32, psum=True) for i in range(2)]
        pot = T("pot", [128, 4, 128], F32, psum=True)

        ph.op("sp", lambda e: e.dma_start(out=mask_f[:], in_=c_mask[:, :]), writes=[mask_f.b], dma=mask_f.b)
        ph.op("dve", lambda e: e.tensor_copy(out=mask[:], in_=mask_f[:]), reads=[mask_f.b], writes=[mask.b])
        v4 = [v.t[:].rearrange("p (t g c) -> p t g c", t=NT, g=2, c=65) for v in vext]
        for i in range(2):
            ph.op("pool", lambda e, i=i: e.memset(v4[i][:, :, :, 64:65], 1.0), writes=[vext[i].sub("ones")])

        units = [(j, d) for j in range(NPAIRS) for d in PATTERNS]

        def issue_loads(u):
            j, d = units[u]
            i = u % 2
            for k, (ts, row0, (rs1, rs2)) in enumerate(pi_dma_specs(d)):
                t0, cnt, tstep = ts
                tsl = slice(t0, t0 + (cnt - 1) * tstep + 1, tstep)
                for (tmt, c0) in ((qtm[i], C_QB + j * 128), (ktm[i], C_KB + j * 128)):
                    ph.op("sp", lambda e, tmt=tmt, c0=c0, tsl=tsl, row0=row0, rs1=rs1, rs2=rs2, cnt=cnt: e.dma_start(
                        out=tmt[:, tsl, :], in_=dram_ap(proj, row0, c0, [(rs1 * IN_W, 128), (rs2 * IN_W, cnt), (1, 128)])),
                        writes=[tmt.sub(k)], dma=tmt.sub(k), after=[tmt.b])
                for g in range(2):
                    c0 = C_VB + j * 128 + g * 64
                    ph.op("sp", lambda e, i=i, g=g, c0=c0, tsl=tsl, row0=row0, rs1=rs1, rs2=rs2, cnt=cnt: e.dma_start(
                        out=v4[i][:, tsl, g, 0:64], in_=dram_ap(proj, row0, c0, [(rs1 * IN_W, 128), (rs2 * IN_W, cnt), (1, 64)])),
                        writes=[vext[i].sub((k, g))], dma=vext[i].sub((k, g)), after=[vext[i].b])
            return len(pi_dma_specs(d))

        def tile_sub(d, t):
            nb = NT // d
            if d == 1:
                return t // 8
            if d == 4:
                return t // nb
            return t % nb

        def do_transposes(u):
            j, d = units[u]
            i = u % 2
            n = 0
            for (tmt, dst) in ((qtm[i], qT[i]), (ktm[i], kT[i])):
                for q4 in range(4):
                    P, eng = (ptr, "dve") if n % 2 == 0 else (ptr2, "act")
                    n += 1
                    for t in range(q4 * 8, q4 * 8 + 8):
                        ph.op("pe", lambda e, t=t, tmt=tmt, P=P: e.transpose(out=P[:, t % 8, :], in_=tmt[:, t, :], identity=ident_b[:]),
                              reads=[tmt.sub(tile_sub(d, t)), tmt.b, ident_b.b], writes=[P.b])
                    if eng == "dve":
                        ph.op("dve", lambda e, q4=q4, dst=dst, P=P: e.tensor_copy(out=dst[:, q4 * 1024:(q4 + 1) * 1024],
                                                                               in_=P[:].rearrange("p a b -> p (a b)")),
                              reads=[P.b], writes=[dst.sub(q4)])
                    else:
                        ph.op("act", lambda e, q4=q4, dst=dst, P=P: e.copy(out=dst[:, q4 * 1024:(q4 + 1) * 1024],
                                                                        in_=P[:].rearrange("p a b -> p (a b)")),
                              reads=[P.b], writes=[dst.sub(q4)])

        def finalize_pair(j):
            for qc in range(S // 512):
                Y = yst[qc % 2]
                for g in range(2):
                    for c in range(4):
                        ph.op("pe", lambda e, c=c, g=g, qc=qc: e.transpose(
                            out=pot[:, c, 0:65], in_=accT[g][0:65, qc * 512 + c * 128:qc * 512 + (c + 1) * 128],
                            identity=ident_f[0:65, 0:65]), reads=[accT[g].b, ident_f.b], writes=[pot.b])
                    a = g
                    PC = potc[g]
                    ph.op("dve", lambda e, PC=PC: e.tensor_copy(out=PC[:, :, :], in_=pot[:, :, 0:65]), reads=[pot.b], writes=[PC.b])
                    ph.op("dve", lambda e, a=a, PC=PC: e.reciprocal(out=rc[a][:], in_=PC[:, :, 64]), reads=[PC.b], writes=[rc[a].b])
                    ph.op("pool", lambda e, a=a, Y=Y, g=g, PC=PC: e.tensor_tensor(out=Y[:, :, g * 64:(g + 1) * 64], in0=PC[:, :, 0:64],
                                                                                 in1=fap(rc[a][:], [(1, 4), (0, 64)]), op=ALU.mult),
                          reads=[PC.b, rc[a].b], writes=[Y.sub(g)])
                ph.op("sp", lambda e, Y=Y, qc=qc, j=j: e.dma_start(
                    out=ybuf[qc * 512:(qc + 1) * 512, 512 + j * 128:512 + (j + 1) * 128].rearrange("(c p) f -> p c f", p=128),
                    in_=Y[:, :, :]), reads=[Y.sub(0), Y.sub(1)], dma=Y.b)

        issue_loads(0)
        cnt = 0
        for u, (j, d) in enumerate(units):
            i = u % 2
            nb = NT // d
            if u + 1 < len(units):
                issue_loads(u + 1)
            do_transposes(u)
            steps = [(g, tq) for g in range(2) for tq in range(NT)]
            N = len(steps)
            info = {}
            LAG = 2
            for n in range(N + LAG):
                if n < N:
                    g, tq = steps[n]
                    r0 = g * 64
                    ib = tq % nb
                    tks = [(tq + o, o + 1) for o in (-1, 0, 1) if 0 <= ib + o < nb]
                    k = cnt % 3
                    cnt += 1
                    for c, (tk, mi) in enumerate(tks):
                        ph.op("pe", lambda e, k=k, c=c, tk=tk, tq=tq, r0=r0, i=i: e.matmul(
                            ps[k][:, c * 128:(c + 1) * 128], lhsT=kT[i][r0:r0 + 64, tk * 128:(tk + 1) * 128],
                            rhs=qT[i][r0:r0 + 64, tq * 128:(tq + 1) * 128], start=True, stop=True),
                            reads=[kT[i].sub(q) for q in range(4)] + [qT[i].sub(q) for q in range(4)], writes=[ps[k].b])
                    w = len(tks) * 128
                    m0 = tks[0][1] * 128
                    ph.op("act", lambda e, k=k, w=w: e.activation(out=pb[k][:, 0:w], in_=ps[k][:, 0:w], func=AF.Exp, scale=SCALE),
                          reads=[ps[k].b], writes=[pb[k].b])
                    ph.op("pool" if cnt % 3 == 0 else "dve", lambda e, k=k, w=w, m0=m0: e.tensor_tensor(out=pbm[k][:, 0:w], in0=pb[k][:, 0:w],
                                                                             in1=mask[:, m0:m0 + w], op=ALU.mult),
                          reads=[pb[k].b, mask.b], writes=[pbm[k].b])
                    info[n] = (k, tks)
                m = n - LAG
                if 0 <= m < N:
                    g, tq = steps[m]
                    k, tks = info[m]
                    a = m % 2
                    for c, (tk, mi) in enumerate(tks):
                        ph.op("pe", lambda e, a=a, c=c, tk=tk, g=g, k=k, i=i, nk=len(tks): e.matmul(
                            pov[a][0:65, 0:128], lhsT=v4[i][:, tk, g, :], rhs=pbm[k][:, c * 128:(c + 1) * 128],
                            start=(c == 0), stop=(c == nk - 1)),
                            reads=[pbm[k].b, vext[i].sub((tile_sub(d, tk), g)), vext[i].sub("ones"), vext[i].b], writes=[pov[a].b])
                    r, ib = tq // nb, tq % nb
                    col0 = r + d * 128 * ib
                    av = fap(accT[g][0:65, col0:col0 + 1], [(d, 128)])
                    if d == PATTERNS[0]:
                        ph.op("dve", lambda e, av=av, a=a: e.tensor_copy(out=av, in_=pov[a][0:65, 0:128]),
                              reads=[pov[a].b], writes=[accT[g].b])
                    else:
                        ph.op("dve", lambda e, av=av, a=a: e.tensor_tensor(out=av, in0=av, in1=pov[a][0:65, 0:128], op=ALU.add),
                              reads=[pov[a].b, accT[g].b], writes=[accT[g].b])
            if d == PATTERNS[-1]:
                finalize_pair(j)
        ph.close()


def phase4(nc, l, x_src, ybuf, w_out, w_mi, w_mo, g_oa, g_ob, g_n2, g_fin, dst, last, ident_b, nt, wi_pre=None):
    ph = Phase(nc, f"p4l{l}")
    with ExitStack() as es:
        T = lambda name, shape, dt, psum=False: Tile(es, nc, f"p4l{l}_{name}", shape, dt, psum)
        wo = [T(f"wo{k}", [128, D], BF16) for k in range(8)]
        wi = wi_pre if wi_pre is not None else [T(f"wi{k}", [128, DFF], BF16) for k in range(8)]
        wo2 = [T(f"wo2{k}", [128, 4, D], BF16) for k in range(8)]
        goa = T("goa", [128, 512], F32)
        gob = T("gob", [128, 512], F32)
        g2 = T("g2", [128, D], F32)
        gfin = T("gfin", [128, D], F32) if last else None
        NB = 2
        y_t = [T(f"y{i}", [128, D], BF16) for i in range(3)]
        x_t = [T(f"x{i}", [128, D], F32) for i in range(3)]
        yn = [T(f"yn{i}", [128, D], BF16) for i in range(NB)]
        yT = [T(f"yT{i}", [128, 8, 128], BF16) for i in range(NB)]
        h2 = [T(f"h2{i}", [128, D], BF16) for i in range(NB)]
        h2T = [T(f"h2T{i}", [128, 8, 128], BF16) for i in range(NB)]
        st = [{k: T(f"{k}{i}", [128, 2], F32) for k in ("ssy", "sdy", "rsy", "ss2", "sd2", "rs2", "ssf", "sdf", "rsf")} for i in range(NB)]
        junk = T("junk", [128, D], BF16)
        r_sb = [T(f"r{i}", [128, 512], F32) for i in range(2)]
        u_bf = [T(f"u{i}", [128, 512], BF16) for i in range(2)]
        uT = T("uT", [128, 32, 128], BF16)
        pT = [T(f"pT{i}", [128, 8, 128], BF16, psum=True) for i in range(2)]
        pw = [T(f"pw{i}", [128, 512], F32, psum=True) for i in range(2)]
        pu = [T(f"pu{i}", [128, 512], F32, psum=True) for i in range(2)]
        po2 = [T(f"po2{i}", [128, 512], F32, psum=True) for i in range(2)]

        wov = w_out[l].rearrange("(kc p) n -> kc p n", p=128)
        wiv = w_mi[l].rearrange("(kc p) n -> kc p n", p=128)
        wo2v = w_mo[l].rearrange("(k4 kc p) n -> k4 p kc n", p=128, kc=4)
        for k in range(8):
            ph.op("pool", lambda e, k=k: e.dma_start(out=wo[k][:], in_=wov[k]), writes=[wo[k].b], dma=wo[k].b)
        if wi_pre is None:
            for k in range(8):
                ph.op("pool", lambda e, k=k: e.dma_start(out=wi[k][:], in_=wiv[k]), writes=[wi[k].b], dma=wi[k].b)
        for k in range(8):
            ph.op("pool", lambda e, k=k: e.dma_start(out=wo2[k][:], in_=wo2v[k]), writes=[wo2[k].b], dma=wo2[k].b)
        ph.op("sp", lambda e: e.dma_start(out=goa[:], in_=g_oa[l:l + 1, :].partition_broadcast(128)), writes=[goa.b], dma=goa.b)
        ph.op("sp", lambda e: e.dma_start(out=gob[:], in_=g_ob[l:l + 1, :].partition_broadcast(128)), writes=[gob.b], dma=gob.b)
        ph.op("sp", lambda e: e.dma_start(out=g2[:], in_=g_n2[l:l + 1, :].partition_broadcast(128)), writes=[g2.b], dma=g2.b)
        if last:
            ph.op("sp", lambda e: e.dma_start(out=gfin[:], in_=g_fin[0:1, :].partition_broadcast(128)), writes=[gfin.b], dma=gfin.b)

        def norm_scale(src, ss, sd, rs, dst_t, gain, width, col0, ncols):
            for j in range(ncols):
                c0 = col0 + j * width
                ph.op("act", lambda e, c0=c0, j=j: e.activation(out=junk[:, c0:c0 + width], in_=src[:, c0:c0 + width],
                                                              func=AF.Square, accum_out=ss[:, j:j + 1]),
                      reads=[src.b], writes=[ss.b])
            rstd_ops(ph, ss, sd, rs, 1.0 / width, ncols)
            for j in range(ncols):
                c0 = col0 + j * width
                ph.op("dve", lambda e, c0=c0, j=j: e.scalar_tensor_tensor(
                    out=dst_t[:, c0:c0 + width], in0=src[:, c0:c0 + width], scalar=rs[:, j:j + 1], in1=gain[j][:, 0:width],
                    op0=ALU.mult, op1=ALU.mult), reads=[src.b, rs.b, gain[j].b], writes=[dst_t.b])

        def A0_load(t):
            ph.op("sp", lambda e: e.dma_start(out=y_t[t % 3][:], in_=ybuf[t * 128:(t + 1) * 128, :]), writes=[y_t[t % 3].b], dma=y_t[t % 3].b)
            ph.op("sp", lambda e: e.dma_start(out=x_t[t % 3][:], in_=x_src[t * 128:(t + 1) * 128, :]), writes=[x_t[t % 3].b], dma=x_t[t % 3].b)

        def A0_norm(t):
            i = t % NB
            norm_scale(y_t[t % 3], st[i]["ssy"], st[i]["sdy"], st[i]["rsy"], yn[i], [goa, gob], 512, 0, 2)

        def A1(t):
            i = t % NB
            transpose8(ph, yn[i], yT[i], pT[0], ident_b, "act")
            for nh in range(2):
                for kc in range(8):
                    ph.op("pe", lambda e, nh=nh, kc=kc: e.matmul(pw[nh][:, :], lhsT=yT[i][:, kc, :], rhs=wo[kc][:, nh * 512:(nh + 1) * 512],
                                                               start=(kc == 0), stop=(kc == 7)),
                          reads=[yT[i].b, wo[kc].b], writes=[pw[nh].b])
            for nh in range(2):
                ph.op("dve", lambda e, nh=nh: e.tensor_tensor(out=x_t[t % 3][:, nh * 512:(nh + 1) * 512], in0=x_t[t % 3][:, nh * 512:(nh + 1) * 512],
                                                             in1=pw[nh][:, :], op=ALU.add),
                      reads=[x_t[t % 3].b, pw[nh].b], writes=[x_t[t % 3].b])
            if DEBUG_XM is not None:
                ph.op("sp", lambda e: e.dma_start(out=DEBUG_XM[t * 128:(t + 1) * 128, :], in_=x_t[t % 3][:]), reads=[x_t[t % 3].b], dma=x_t[t % 3].sub("dbg"))
            norm_scale(x_t[t % 3], st[i]["ss2"], st[i]["sd2"], st[i]["rs2"], h2[i], [g2], D, 0, 1)

        def A2(t):
            i = t % NB
            transpose8(ph, h2[i], h2T[i], pT[1], ident_b, "act")

        def u_transposes(c):
            x = (c // 2) % 2
            for q in range(4):
                ph.op("pe", lambda e, q=q: e.transpose(out=pT[x][:, (c % 2) * 4 + q, :], in_=u_bf[c % 2][:, q * 128:(q + 1) * 128],
                                                      identity=ident_b[:]),
                      reads=[u_bf[c % 2].b, ident_b.b], writes=[pT[x].b])
            if c % 2 == 1:
                ph.op("dve", lambda e: e.tensor_copy(out=uT[:, (c - 1) * 4:(c + 1) * 4, :], in_=pT[x][:]),
                      reads=[pT[x].b], writes=[uT.sub(c // 2)])

        def B(t, upto=8):
            i = t % NB
            for c in range(upto):
                for kc in range(8):
                    ph.op("pe", lambda e, c=c, kc=kc: e.matmul(pu[c % 2][:, :], lhsT=h2T[i][:, kc, :], rhs=wi[kc][:, c * 512:(c + 1) * 512],
                                                             start=(kc == 0), stop=(kc == 7)),
                          reads=[h2T[i].b, wi[kc].b], writes=[pu[c % 2].b])
                ph.op("act", lambda e, c=c: e.activation(out=r_sb[c % 2][:], in_=pu[c % 2][:], func=AF.Relu),
                      reads=[pu[c % 2].b], writes=[r_sb[c % 2].b])
                ph.op("dve", lambda e, c=c: e.tensor_tensor(out=u_bf[c % 2][:], in0=r_sb[c % 2][:], in1=r_sb[c % 2][:], op=ALU.mult),
                      reads=[r_sb[c % 2].b], writes=[u_bf[c % 2].b])
                if c >= 1:
                    u_transposes(c - 1)

        def C_mm(t):
            i = t % NB
            for nh in range(2):
                for kc in range(32):
                    ph.op("pe", lambda e, nh=nh, kc=kc: e.matmul(po2[nh][:, :], lhsT=uT[:, kc, :],
                                                               rhs=wo2[kc // 4][:, kc % 4, nh * 512:(nh + 1) * 512],
                                                               start=(kc == 0), stop=(kc == 31)),
                          reads=[uT.sub(kc // 8), wo2[kc // 4].b], writes=[po2[nh].b])
            for nh in range(2):
                ph.op("dve", lambda e, nh=nh: e.tensor_tensor(out=x_t[t % 3][:, nh * 512:(nh + 1) * 512], in0=x_t[t % 3][:, nh * 512:(nh + 1) * 512],
                                                             in1=po2[nh][:, :], op=ALU.add),
                      reads=[x_t[t % 3].b, po2[nh].b], writes=[x_t[t % 3].b])

        def C_fin(t):
            i = t % NB
            if last:
                norm_scale(x_t[t % 3], st[i]["ssf"], st[i]["sdf"], st[i]["rsf"], x_t[t % 3], [gfin], D, 0, 1)
            ph.op("sp", lambda e: e.dma_start(out=dst[t * 128:(t + 1) * 128, :], in_=x_t[t % 3][:]), reads=[x_t[t % 3].b], dma=x_t[t % 3].b)

        if SIMPLE_P4:
            for t in range(nt):
                A0_load(t)
                A0_norm(t)
                A1(t)
                A2(t)
                B(t)
                u_transposes(7)
                C_mm(t)
                C_fin(t)
        else:
            A0_load(0)
            if nt > 1:
                A0_load(1)
            A0_norm(0)
            A1(0)
            A2(0)
            for t in range(nt):
                if t + 2 < nt:
                    A0_load(t + 2)
                if t + 1 < nt:
                    A0_norm(t + 1)
                B(t)
                if t + 1 < nt:
                    A1(t + 1)
                u_transposes(7)
                C_mm(t)
                if t + 1 < nt:
                    A2(t + 1)
                C_fin(t)
        ph.close()


def _perm_in():
    qa = []
    for j in range(4):
        for h in (j, j + 4):
            qa += list(range(h * 64, h * 64 + 64))
    return np.array(qa + list(range(512, IN_W)), dtype=np.int64)


def _consts():
    ident = np.eye(128, dtype=np.float32)
    t = np.arange(S)
    half = 16
    fr = (10000.0 ** (-np.arange(half, dtype=np.float32) / half)).astype(np.float32)
    ang_r = (t // 64).astype(np.float32)[:, None] * fr[None, :]
    ang_c = (t % 64).astype(np.float32)[:, None] * fr[None, :]
    cr, sr, cc, sc = np.cos(ang_r), np.sin(ang_r), np.cos(ang_c), np.sin(ang_c)
    ropeA = np.concatenate([cr, cr, cc, cc, -sr, sr, -sc, sc], axis=1).astype(np.float32)
    fb = (500000.0 ** (-np.arange(8, dtype=np.float32) / 8)).astype(np.float32)
    ang = t.astype(np.float32)[:, None] * fb[None, :]
    cb, sb = np.cos(ang), np.sin(ang)
    ropeB = np.concatenate([cb, cb, -sb, sb], axis=1).astype(np.float32)
    kk = np.arange(128)[:, None]
    qq = np.arange(128)[None, :]
    mask = np.concatenate([(np.abs(o + kk - qq) <= 64).astype(np.float32) for o in (-128, 0, 128)], axis=1)
    return ident, ropeA, ropeB, mask


def make_in_maps(inputs, n_cores=8):
    perm = _perm_in()
    ident, ropeA, ropeB, mask = _consts()
    rowperm = np.concatenate([perm[:512], np.arange(512, 1024)])
    w_in = np.ascontiguousarray(np.asarray(inputs["w_in"], dtype=np.float32)[:, :, perm])
    shared = {
        "w_in": w_in,
        "w_out": np.ascontiguousarray(np.asarray(inputs["w_out"], dtype=np.float32)[:, rowperm, :]),
        "w_mlp_in": np.ascontiguousarray(inputs["w_mlp_in"], dtype=np.float32),
        "w_mlp_out": np.ascontiguousarray(inputs["w_mlp_out"], dtype=np.float32),
        "norm1": np.ascontiguousarray(inputs["norm1"], dtype=np.float32),
        "norm2": np.ascontiguousarray(inputs["norm2"], dtype=np.float32),
        "q_norm": np.ascontiguousarray(inputs["q_norm"], dtype=np.float32),
        "k_norm": np.ascontiguousarray(inputs["k_norm"], dtype=np.float32),
        "out_norm_a": np.ascontiguousarray(np.asarray(inputs["out_norm_a"], dtype=np.float32)[:, perm[:512]]),
        "out_norm_b": np.ascontiguousarray(inputs["out_norm_b"], dtype=np.float32),
        "final_norm": np.ascontiguousarray(np.asarray(inputs["final_norm"], dtype=np.float32).reshape(1, D)),
        "c_ident": ident, "c_ropeA": ropeA, "c_ropeB": ropeB, "c_mask": mask,
    }
    x = np.asarray(inputs["x"], dtype=np.float32)
    maps = []
    for c in range(n_cores):
        m = dict(shared)
        m["x"] = np.ascontiguousarray(x[(c // 2) % x.shape[0]])
        maps.append(m)
    return maps


def kernel(**inputs):
    nc = build_program()
    in_maps = make_in_maps(inputs, 8)
    res = run_bass_kernel_spmd(nc, in_maps, core_ids=list(range(8)))
    outs = [np.asarray(res.results[2 * b]["out"], dtype=np.float32) for b in range(4)]
    return np.stack(outs, axis=0)
```

```python
import numpy as np
from contextlib import ExitStack
import concourse.bass as bass
import concourse.mybir as mybir
from concourse.bass_utils import run_bass_kernel_spmd

F32 = mybir.dt.float32
BF16 = mybir.dt.bfloat16
AF = mybir.ActivationFunctionType
ALU = mybir.AluOpType
AX = mybir.AxisListType

S = 4096
D = 1024
NT = S // 128
HD = 64
DEPTH = 4
IN_W = 2304
DFF = 4096
EPS = 1e-6
SCALE = HD ** -0.5
SELF_SYNC = True
PATTERNS = (1, 4, 16)
NPAIRS = 4
DEBUG_XM = None
DUMP = None
SIMPLE_P4 = False

C_QA, C_KA, C_VA, C_QB, C_KB, C_VB = 0, 512, 640, 768, 1280, 1792


def fap(ap, dims):
    return bass.AP(ap.tensor, ap.offset, [list(ap.ap[0])] + [list(d) for d in dims])


class _Sem:
    __slots__ = ("h", "total", "owner")

    def __init__(self, h, owner=None):
        self.h = h
        self.total = 0
        self.owner = owner


class Buf:
    __slots__ = ("name", "w", "r", "dsem")

    def __init__(self, name):
        self.name = name
        self.w = None
        self.r = {}
        self.dsem = None


class SemPool:
    def __init__(self, es, nc, n):
        self.free = [es.enter_context(nc.semaphore(f"sp{i}")) for i in range(n)]

    def get(self):
        return self.free.pop(0)

    def put(self, hs):
        self.free = list(hs) + self.free


POOL = None
SWSEMS = []


class Phase:
    ENGS = (("pe", "tensor"), ("act", "scalar"), ("dve", "vector"), ("pool", "gpsimd"), ("sp", "sync"))

    def __init__(self, nc, name):
        self.nc = nc
        self.name = name
        self.es = ExitStack()
        self.q = {e: [] for e, _ in self.ENGS}
        self.waited = {e: {} for e, _ in self.ENGS}
        self.esem = {}
        for e in ("pe", "act", "dve", "pool"):
            self.esem[e] = _Sem(POOL.get(), self)
        self.dsems = []
        self.swsems = []
        self.nsw = 0

    def _dma_sem(self, buf, sw=False):
        if buf.dsem is None or buf.dsem[0] is not self:
            if sw:
                s = SWSEMS[self.nsw]
                self.nsw += 1
                s.owner = self
                self.swsems.append(s)
            else:
                s = _Sem(POOL.get(), self)
                self.dsems.append(s)
            buf.dsem = (self, s)
        return buf.dsem[1]

    def op(self, eng, fn, reads=(), writes=(), dma=None, after=()):
        deps = []
        for b in after:
            if b.w is not None:
                deps.append(b.w)
            deps.extend(b.r.values())
        for b in reads:
            if b.w is not None:
                deps.append(b.w)
        for b in writes:
            if b.w is not None:
                deps.append(b.w)
            deps.extend(b.r.values())
        waits = {}
        for (s, v, src, phs) in deps:
            if phs is not self:
                continue
            if src == eng and (eng == "pe" or not SELF_SYNC):
                continue
            if self.waited[eng].get(s, 0) >= v:
                continue
            if src == "dma":
                assert v == s.total, f"DMA semaphore reuse hazard in {self.name}"
            if waits.get(s, 0) < v:
                waits[s] = v
        for s, v in waits.items():
            self.waited[eng][s] = v
        if dma is not None:
            sem = self._dma_sem(dma, sw=(eng == "pool"))
            sem.total += 16
            ev = (sem, sem.total, "dma", self)
            amt = 16
        else:
            sem = self.esem[eng]
            sem.total += 1
            ev = (sem, sem.total, eng, self)
            amt = 1
        self.q[eng].append((list(waits.items()), fn, sem, amt))
        for b in reads:
            old = b.r.get(ev[0])
            if old is None or old[3] is not self or old[1] < ev[1]:
                b.r[ev[0]] = ev
        for b in writes:
            b.w = ev
            b.r = {}
        return ev

    def close(self):
        fin = [(s, s.total) for s in self.dsems + self.swsems if s.total > 0 and self.waited["sp"].get(s, 0) < s.total]
        self.q["sp"].append((fin, None, None, 0))
        allsems = list(self.esem.values()) + self.dsems
        with self.nc.Block() as cb:
            def fclear(e):
                for s in allsems:
                    e.sem_clear(s.h)
            cb.gpsimd(fclear)
        with self.nc.Block() as block:
            for eng, attr in self.ENGS:
                items = self.q[eng]
                if not items:
                    continue

                def f(e, items=items):
                    for waits, fn, sem, amt in items:
                        for (s, v) in waits:
                            e.wait_ge(s.h, v)
                        if fn is not None:
                            fn(e).then_inc(sem.h, amt)
                getattr(block, attr)(f)
        POOL.put([s.h for s in allsems])
        self.es.close()


class Tile:
    def __init__(self, es, nc, name, shape, dtype, psum=False):
        if psum:
            self.t = es.enter_context(nc.psum_tensor(name, shape, dtype))
        else:
            self.t = es.enter_context(nc.sbuf_tensor(name, shape, dtype))
        self.b = Buf(name)

        self.subs = {}

    def sub(self, key):
        if key not in self.subs:
            self.subs[key] = Buf(f"{self.b.name}_{key}")
        return self.subs[key]

    def __getitem__(self, k):
        return self.t[k]


def build_program(n_layers=DEPTH, debug=False, nt1=NT, phases=(1, 2, 3, 4)):
    nc = bass.Bass("TRN2", target_bir_lowering=False)
    dk = "ExternalOutput" if debug else "Internal"
    x_in = nc.dram_tensor("x", [S, D], F32, kind="ExternalInput").ap()
    w_in = nc.dram_tensor("w_in", [DEPTH, D, IN_W], F32, kind="ExternalInput").ap()
    w_out = nc.dram_tensor("w_out", [DEPTH, D, D], F32, kind="ExternalInput").ap()
    w_mi = nc.dram_tensor("w_mlp_in", [DEPTH, D, DFF], F32, kind="ExternalInput").ap()
    w_mo = nc.dram_tensor("w_mlp_out", [DEPTH, DFF, D], F32, kind="ExternalInput").ap()
    g_n1 = nc.dram_tensor("norm1", [DEPTH, D], F32, kind="ExternalInput").ap()
    g_n2 = nc.dram_tensor("norm2", [DEPTH, D], F32, kind="ExternalInput").ap()
    g_q = nc.dram_tensor("q_norm", [DEPTH, HD], F32, kind="ExternalInput").ap()
    g_k = nc.dram_tensor("k_norm", [DEPTH, HD], F32, kind="ExternalInput").ap()
    g_oa = nc.dram_tensor("out_norm_a", [DEPTH, 512], F32, kind="ExternalInput").ap()
    g_ob = nc.dram_tensor("out_norm_b", [DEPTH, 512], F32, kind="ExternalInput").ap()
    g_fin = nc.dram_tensor("final_norm", [1, D], F32, kind="ExternalInput").ap()
    c_ident = nc.dram_tensor("c_ident", [128, 128], F32, kind="ExternalInput").ap()
    c_ropeA = nc.dram_tensor("c_ropeA", [S, 128], F32, kind="ExternalInput").ap()
    c_ropeB = nc.dram_tensor("c_ropeB", [S, 32], F32, kind="ExternalInput").ap()
    c_mask = nc.dram_tensor("c_mask", [128, 384], F32, kind="ExternalInput").ap()
    out = nc.dram_tensor("out", [S, D], F32, kind="ExternalOutput").ap()
    proj = nc.dram_tensor("proj", [S, IN_W], BF16, kind=dk).ap()
    ybuf = nc.dram_tensor("ybuf", [S, D], BF16, kind=dk).ap()
    xres = nc.dram_tensor("xres", [S, D], F32, kind=dk).ap()

    global DEBUG_XM
    DEBUG_XM = xres if (debug and n_layers == 1) else None
    global POOL, SWSEMS
    with ExitStack() as ges:
        POOL = SemPool(ges, nc, 64)
        SWSEMS[:] = [_Sem(ges.enter_context(nc.semaphore(f"sw{i}"))) for i in range(24)]
        ident_f = Tile(ges, nc, "ident_f", [128, 128], F32)
        ident_b = Tile(ges, nc, "ident_b", [128, 128], BF16)
        wi_g = [Tile(ges, nc, f"wi_g{k}", [128, DFF], BF16) for k in range(8)]
        ph = Phase(nc, "c0")
        ph.op("sp", lambda e: e.dma_start(out=ident_f[:], in_=c_ident[:, :]), writes=[ident_f.b], dma=ident_f.b)
        ph.op("pool", lambda e: e.dma_start(out=ident_b[:], in_=c_ident[:, :]), writes=[ident_b.b], dma=ident_b.b)
        ph.close()

        for l in range(n_layers):
            x_src = x_in if l == 0 else xres
            if 1 in phases:
                phase1(nc, l, x_src, w_in, g_n1, g_q, g_k, c_ropeA, c_ropeB, proj, ident_b, nt1)
            if 2 in phases:
                phase2(nc, l, proj, ybuf, ident_b, ident_f, wi_g if 4 in phases else None, w_mi)
            if 3 in phases:
                phase3(nc, l, proj, ybuf, c_mask, ident_b, ident_f)
            if 4 in phases:
                last = (l == n_layers - 1)
                phase4(nc, l, x_src, ybuf, w_out, w_mi, w_mo, g_oa, g_ob, g_n2, g_fin,
                       out if last else xres, last, ident_b, nt1, wi_g if 2 in phases else None)
    return nc


def rstd_ops(ph, ss, tmp, rs, scale, n=None):
    sl = slice(None) if n is None else slice(0, n)
    ph.op("dve", lambda e: e.tensor_scalar(out=tmp[:, sl], in0=ss[:, sl], scalar1=scale, scalar2=EPS, op0=ALU.mult, op1=ALU.add),
          reads=[ss.b], writes=[tmp.b])
    ph.op("act", lambda e: e.activation(out=tmp[:, sl], in_=tmp[:, sl], func=AF.Sqrt), reads=[tmp.b], writes=[tmp.b])
    ph.op("dve", lambda e: e.reciprocal(out=rs[:, sl], in_=tmp[:, sl]), reads=[tmp.b], writes=[rs.b])


def transpose8(ph, src, dst, pT, ident_b, evac_eng):
    for kc in range(8):
        ph.op("pe", lambda e, kc=kc: e.transpose(out=pT[:, kc, :], in_=src[:, kc * 128:(kc + 1) * 128], identity=ident_b[:]),
              reads=[src.b, ident_b.b], writes=[pT.b])
    if evac_eng == "act":
        ph.op("act", lambda e: e.copy(out=dst[:], in_=pT[:]), reads=[pT.b], writes=[dst.b])
    else:
        ph.op("dve", lambda e: e.tensor_copy(out=dst[:], in_=pT[:]), reads=[pT.b], writes=[dst.b])


def phase1(nc, l, x_src, w_in, g_n1, g_q, g_k, c_ropeA, c_ropeB, proj, ident_b, nt):
    ph = Phase(nc, f"p1l{l}")
    with ExitStack() as es:
        T = lambda name, shape, dt, psum=False: Tile(es, nc, f"p1l{l}_{name}", shape, dt, psum)
        wk = [T(f"w{kc}", [128, IN_W], BF16) for kc in range(8)]
        g1 = T("g1", [128, D], F32)
        gq = T("gq", [128, HD], F32)
        gk = T("gk", [128, HD], F32)
        ropeA = T("ropeA", [128, NT, 128], F32)
        ropeB = T("ropeB", [128, NT, 32], F32)
        NB = 2
        x_t = [T(f"x{i}", [128, D], F32) for i in range(3)]
        junk = [T(f"junk{i}", [128, D], BF16) for i in range(NB)]
        ss = [T(f"ss{i}", [128, 1], F32) for i in range(NB)]
        sd = [T(f"sd{i}", [128, 1], F32) for i in range(NB)]
        rs = [T(f"rs{i}", [128, 1], F32) for i in range(NB)]
        h_b = [T(f"h{i}", [128, D], BF16) for i in range(NB)]
        hT = [T(f"hT{i}", [128, 8, 128], BF16) for i in range(NB)]
        po = [T(f"po{i}", [128, IN_W], BF16) for i in range(NB)]
        sq = [T(f"sq{i}", [128, 640], F32) for i in range(NB)]
        s8 = [T(f"s8{i}", [128, 10], F32) for i in range(NB)]
        d8 = [T(f"d8{i}", [128, 10], F32) for i in range(NB)]
        r8 = [T(f"r8{i}", [128, 10], F32) for i in range(NB)]
        qn = [T(f"qn{i}", [128, 640], F32) for i in range(NB)]
        t1 = [T(f"t1{i}", [128, 640], F32) for i in range(NB)]
        t2 = [T(f"t2{i}", [128, 640], F32) for i in range(NB)]
        u1 = [T(f"u1{i}", [128, 256], F32) for i in range(NB)]
        u2 = [T(f"u2{i}", [128, 256], F32) for i in range(NB)]
        f0 = [T(f"f0{i}", [128, 640], F32) for i in range(NB)]
        f1 = f0
        fb = [T(f"fb{i}", [128, 256], F32) for i in range(NB)]
        pT = [T(f"pT{i}", [128, 8, 128], BF16, psum=True) for i in range(1)]
        pp = [T(f"pp{i}", [128, 512], F32, psum=True) for i in range(5)]

        wv = w_in[l].rearrange("(kc p) n -> kc p n", p=128)
        for kc in range(8):
            ph.op("pool", lambda e, kc=kc: e.dma_start(out=wk[kc][:], in_=wv[kc]), writes=[wk[kc].b], dma=wk[kc].b)
        ph.op("sp", lambda e: e.dma_start(out=g1[:], in_=g_n1[l:l + 1, :].partition_broadcast(128)), writes=[g1.b], dma=g1.b)
        ph.op("sp", lambda e: e.dma_start(out=gq[:], in_=g_q[l:l + 1, :].partition_broadcast(128)), writes=[gq.b], dma=gq.b)
        ph.op("sp", lambda e: e.dma_start(out=gk[:], in_=g_k[l:l + 1, :].partition_broadcast(128)), writes=[gk.b], dma=gk.b)
        ph.op("sp", lambda e: e.dma_start(out=ropeA[:], in_=c_ropeA.rearrange("(t p) c -> p t c", p=128)), writes=[ropeA.b], dma=ropeA.b)
        ph.op("sp", lambda e: e.dma_start(out=ropeB[:], in_=c_ropeB.rearrange("(t p) c -> p t c", p=128)), writes=[ropeB.b], dma=ropeB.b)

        groups = [(C_QA, 512), (C_KA, 256), (C_QB, 512), (C_KB, 512), (C_VB, 512)]

        def slot(t):
            i = t % NB
            return (x_t[t % 3], junk[i], ss[i], sd[i], rs[i], h_b[i], hT[i], po[i], sq[i], s8[i], d8[i], r8[i], qn[i], t1[i], t2[i], u1[i], u2[i])

        def load_x(t):
            X = x_t[t % 3]
            ph.op("sp", lambda e, X=X, t=t: e.dma_start(out=X[:], in_=x_src[t * 128:(t + 1) * 128, :]), writes=[X.b], dma=X.b)

        def stage_N(t):
            X, J, SS, SD, RS, H, HT, PO, SQ, S8, D8, R8, QN, T1, T2, U1, U2 = slot(t)
            ph.op("act", lambda e, X=X, J=J, SS=SS: e.activation(out=J[:], in_=X[:], func=AF.Square, accum_out=SS[:]),
                  reads=[X.b], writes=[J.b, SS.b])
            rstd_ops(ph, SS, SD, RS, 1.0 / D)
            ph.op("dve", lambda e, X=X, RS=RS, H=H: e.scalar_tensor_tensor(out=H[:], in0=X[:], scalar=RS[:, 0:1], in1=g1[:],
                                                                         op0=ALU.mult, op1=ALU.mult),
                  reads=[X.b, RS.b, g1.b], writes=[H.b])

        def stage_T(t):
            X, J, SS, SD, RS, H, HT, PO, SQ, S8, D8, R8, QN, T1, T2, U1, U2 = slot(t)
            transpose8(ph, H, HT, pT[0], ident_b, "dve")

        def stage_M(t):
            X, J, SS, SD, RS, H, HT, PO, SQ, S8, D8, R8, QN, T1, T2, U1, U2 = slot(t)
            for gi, (c0, wd) in enumerate(groups):
                for kc in range(8):
                    ph.op("pe", lambda e, gi=gi, c0=c0, wd=wd, kc=kc, HT=HT: e.matmul(
                        pp[gi][:, 0:wd], lhsT=HT[:, kc, :], rhs=wk[kc][:, c0:c0 + wd], start=(kc == 0), stop=(kc == 7)),
                        reads=[HT.b, wk[kc].b], writes=[pp[gi].b])

        def stage_Pe_a(t):
            X, J, SS, SD, RS, H, HT, PO, SQ, S8, D8, R8, QN, T1, T2, U1, U2 = slot(t)
            i = t % NB
            F0, F1, FB = f0[i], f1[i], fb[i]
            ph.op("act", lambda e: e.copy(out=F0[:, 0:512], in_=pp[0][:, 0:512]), reads=[pp[0].b], writes=[F0.b])
            ph.op("dve", lambda e: e.tensor_copy(out=F0[:, 512:640], in_=pp[1][:, 0:128]), reads=[pp[1].b], writes=[F1.b])
            ph.op("dve", lambda e: e.tensor_copy(out=PO[:, C_VA:C_VA + 128], in_=pp[1][:, 128:256]), reads=[pp[1].b], writes=[PO.sub('va')])
            ph.op("act", lambda e: e.copy(out=PO[:, C_QB:C_QB + 512], in_=pp[2][:, :]), reads=[pp[2].b], writes=[PO.sub(2)])
            ph.op("act", lambda e: e.copy(out=PO[:, C_KB:C_KB + 512], in_=pp[3][:, :]), reads=[pp[3].b], writes=[PO.sub(3)])
            ph.op("act", lambda e: e.copy(out=PO[:, C_VB:C_VB + 512], in_=pp[4][:, :]), reads=[pp[4].b], writes=[PO.sub('vb')])

        def stage_Pe_b(t):
            X, J, SS, SD, RS, H, HT, PO, SQ, S8, D8, R8, QN, T1, T2, U1, U2 = slot(t)
            i = t % NB
            F0, F1, FB = f0[i], f1[i], fb[i]
            ph.op("dve", lambda e: e.tensor_copy(out=FB[:, 0:128].rearrange("p (h d) -> p h d", d=16), in_=fap(pp[2][:, 0:16], [(64, 8), (1, 16)])),
                  reads=[pp[2].b, PO.sub(2)], writes=[FB.sub(2)])
            ph.op("dve", lambda e: e.tensor_copy(out=FB[:, 128:256].rearrange("p (h d) -> p h d", d=16), in_=fap(pp[3][:, 0:16], [(64, 8), (1, 16)])),
                  reads=[pp[3].b, PO.sub(3)], writes=[FB.sub(3)])

        def stage_Pm(t):
            X, J, SS, SD, RS, H, HT, PO, SQ, S8, D8, R8, QN, T1, T2, U1, U2 = slot(t)
            i = t % NB
            F0, F1, FB = f0[i], f1[i], fb[i]
            ph.op("act", lambda e: e.activation(out=SQ[:, 0:640], in_=F0[:, 0:640], func=AF.Square),
                  reads=[F0.b, F1.b], writes=[SQ.b])
            ph.op("dve", lambda e: e.tensor_reduce(out=S8[:], in_=SQ[:].rearrange("p (h d) -> p h d", d=HD), axis=AX.X, op=ALU.add),
                  reads=[SQ.b], writes=[S8.b])
            rstd_ops(ph, S8, D8, R8, 1.0 / HD)
            ph.op("dve", lambda e: e.tensor_tensor(out=QN[:, 0:640].rearrange("p (h d) -> p h d", d=HD),
                                                   in0=F0[:, 0:640].rearrange("p (h d) -> p h d", d=HD),
                                                   in1=fap(R8[:, 0:10], [(1, 10), (0, HD)]), op=ALU.mult),
                  reads=[F0.b, F1.b, R8.b], writes=[QN.b])
            ph.op("dve", lambda e: e.tensor_tensor(out=QN[:, 0:512].rearrange("p (h d) -> p h d", d=HD),
                                                    in0=QN[:, 0:512].rearrange("p (h d) -> p h d", d=HD),
                                                    in1=fap(gq[:], [(0, 8), (1, HD)]), op=ALU.mult),
                  reads=[QN.b, gq.b], writes=[QN.b])
            ph.op("dve", lambda e: e.tensor_tensor(out=QN[:, 512:640].rearrange("p (h d) -> p h d", d=HD),
                                                    in0=QN[:, 512:640].rearrange("p (h d) -> p h d", d=HD),
                                                    in1=fap(gk[:], [(0, 2), (1, HD)]), op=ALU.mult),
                  reads=[QN.b, gk.b], writes=[QN.b])
            cosA = fap(ropeA[:, t, 0:64], [(0, 10), (1, 64)])
            ph.op("dve", lambda e: e.tensor_tensor(out=T1[:].rearrange("p (h d) -> p h d", d=HD),
                                                   in0=QN[:].rearrange("p (h d) -> p h d", d=HD), in1=cosA, op=ALU.mult),
                  reads=[QN.b, ropeA.b], writes=[T1.b])
            for half in range(2):
                sn = fap(ropeA[:, t, 64 + half * 16:64 + half * 16 + 16], [(0, 10), (32, 2), (1, 16)])
                dst = fap(T2[:, half * 16:half * 16 + 16], [(64, 10), (32, 2), (1, 16)])
                src = fap(QN[:, (1 - half) * 16:(1 - half) * 16 + 16], [(64, 10), (32, 2), (1, 16)])
                ph.op("pool", lambda e, dst=dst, src=src, sn=sn: e.tensor_tensor(out=dst, in0=src, in1=sn, op=ALU.mult),
                      reads=[QN.b, ropeA.b], writes=[T2.b])
            ph.op("dve", lambda e: e.tensor_tensor(out=PO[:, C_QA:C_QA + 640], in0=T1[:], in1=T2[:], op=ALU.add),
                  reads=[T1.b, T2.b], writes=[PO.sub('qa')])
            for gi, c0, off in ((2, C_QB, 0), (3, C_KB, 128)):
                xin = FB[:, off:off + 128].rearrange("p (h d) -> p h d", d=16)
                cosB = fap(ropeB[:, t, 0:16], [(0, 8), (1, 16)])
                ph.op("pool", lambda e, xin=xin, cosB=cosB, off=off: e.tensor_tensor(
                    out=U1[:, off:off + 128].rearrange("p (h d) -> p h d", d=16), in0=xin, in1=cosB, op=ALU.mult),
                    reads=[FB.sub(gi), ropeB.b], writes=[U1.sub(gi)])
                for half in range(2):
                    xs = fap(FB[:, off + (1 - half) * 8:off + (1 - half) * 8 + 8], [(16, 8), (1, 8)])
                    sn = fap(ropeB[:, t, 16 + half * 8:16 + half * 8 + 8], [(0, 8), (1, 8)])
                    dst = fap(U2[:, off + half * 8:off + half * 8 + 8], [(16, 8), (1, 8)])
                    ph.op("pool", lambda e, dst=dst, xs=xs, sn=sn: e.tensor_tensor(out=dst, in0=xs, in1=sn, op=ALU.mult),
                          reads=[FB.sub(gi), ropeB.b], writes=[U2.sub(gi)])
                ph.op("dve", lambda e, c0=c0, off=off: e.tensor_tensor(
                    out=fap(PO[:, c0:c0 + 16], [(64, 8), (1, 16)]),
                    in0=U1[:, off:off + 128].rearrange("p (h d) -> p h d", d=16),
                    in1=U2[:, off:off + 128].rearrange("p (h d) -> p h d", d=16), op=ALU.add),
                    reads=[U1.sub(gi), U2.sub(gi)], writes=[PO.sub(gi)])
            ph.op("sp", lambda e: e.dma_start(out=proj[t * 128:(t + 1) * 128, :], in_=PO[:]),
                  reads=[PO.sub('qa'), PO.sub('va'), PO.sub(2), PO.sub(3), PO.sub('vb')], dma=PO.b)

        for t0 in range(min(3, nt)):
            load_x(t0)
        stage_N(0)
        stage_T(0)
        stage_M(0)
        if nt > 1:
            stage_N(1)
        for k in range(nt):
            if k + 3 < nt:
                load_x(k + 3)
            stage_Pe_a(k)
            if k + 1 < nt:
                stage_T(k + 1)
            stage_Pe_b(k)
            if k + 1 < nt:
                stage_M(k + 1)
            if k + 2 < nt:
                stage_N(k + 2)
            stage_Pm(k)
        ph.close()


def load_transposed(ph, proj, c0, tm, dstT, ptr, ident_b, row_sel=None):
    pv = proj.rearrange("(t p) c -> p t c", p=128)
    for q4 in range(4):
        ph.op("sp", lambda e, q4=q4: e.dma_start(out=tm[:, q4 * 8:(q4 + 1) * 8, :], in_=pv[:, q4 * 8:(q4 + 1) * 8, c0:c0 + 128]),
              writes=[tm.sub(q4)], dma=tm.sub(q4))
    for q4 in range(4):
        for t in range(q4 * 8, q4 * 8 + 8):
            ph.op("pe", lambda e, t=t: e.transpose(out=ptr[:, t % 8, :], in_=tm[:, t, :], identity=ident_b[:]),
                  reads=[tm.sub(q4), ident_b.b], writes=[ptr.b])
        ph.op("dve", lambda e, q4=q4: e.tensor_copy(out=dstT[:, q4 * 1024:(q4 + 1) * 1024], in_=ptr[:].rearrange("p a b -> p (a b)")),
              reads=[ptr.b], writes=[dstT.b])


def phase2(nc, l, proj, ybuf, ident_b, ident_f, wi_g, w_mi):
    ph = Phase(nc, f"p2l{l}")
    with ExitStack() as es:
        T = lambda name, shape, dt, psum=False: Tile(es, nc, f"p2l{l}_{name}", shape, dt, psum)
        kT = T("kT", [128, S], BF16)
        vext = T("vext", [128, NT * 2 * 65], BF16)
        qT = [T(f"qT{j}", [128, S], BF16) for j in range(4)]
        tm = [T(f"tm{i}", [128, NT, 128], BF16) for i in range(2)]
        pb = [T(f"pb{i}", [128, 1024], BF16) for i in range(3)]
        oT = [T(f"oT{i}", [128, 1024], F32) for i in range(2)]
        rc = [T(f"rc{i}", [128, 4], F32) for i in range(2)]
        yst = [T(f"yst{i}", [128, 4, 512], BF16) for i in range(2)]
        ptr = T("ptr", [128, 8, 128], BF16, psum=True)
        ps = [T(f"ps{i}", [128, 1024], F32, psum=True) for i in range(2)]
        pov = T("pov", [128, 1024], F32, psum=True)
        pot = T("pot", [128, 4, 128], F32, psum=True)

        v4 = vext.t[:].rearrange("p (t g c) -> p t g c", t=NT, g=2, c=65)
        ph.op("pool", lambda e: e.memset(v4[:, :, :, 64:65], 1.0), writes=[vext.sub("ones")])
        if wi_g is not None:
            wiv = w_mi[l].rearrange("(kc p) n -> kc p n", p=128)
            for k in range(8):
                ph.op("pool", lambda e, k=k: e.dma_start(out=wi_g[k][:], in_=wiv[k]), writes=[wi_g[k].b], dma=wi_g[k].b)
        pv = proj.rearrange("(t p) c -> p t c", p=128)
        for q4 in range(4):
            for g in range(2):
                ph.op("sp", lambda e, q4=q4, g=g: e.dma_start(
                    out=v4[:, q4 * 8:(q4 + 1) * 8, g, 0:64],
                    in_=pv[:, q4 * 8:(q4 + 1) * 8, C_VA + g * 64:C_VA + g * 64 + 64]),
                    writes=[vext.sub((q4, g))], dma=vext.sub((q4, g)))
        load_transposed(ph, proj, C_KA, tm[0], kT, ptr, ident_b)
        for j in range(4):
            load_transposed(ph, proj, C_QA + j * 128, tm[(j + 1) % 2], qT[j], ptr, ident_b)

        steps = [(qc, j, kt) for qc in range(S // 512) for j in range(4) for kt in range(NT)]
        N = len(steps)
        deferred = {}

        def finalize_pe(grp, qc, j):
            a = grp % 2
            Y = yst[qc % 2]
            for g in range(2):
                for c in range(4):
                    ph.op("pe", lambda e, c=c, g=g: e.transpose(out=pot[:, c, 0:65],
                                                              in_=oT[a][0:65, g * 512 + c * 128:g * 512 + (c + 1) * 128],
                                                              identity=ident_f[0:65, 0:65]),
                          reads=[oT[a].b, ident_f.b], writes=[pot.b])
                ph.op("dve", lambda e, g=g: e.reciprocal(out=rc[g][:], in_=pot[:, :, 64]), reads=[pot.b], writes=[rc[g].b])
                hc = (2 * j + g) * 64
                ph.op("dve", lambda e, g=g, hc=hc: e.tensor_tensor(out=Y[:, :, hc:hc + 64], in0=pot[:, :, 0:64],
                                                                   in1=fap(rc[g][:], [(1, 4), (0, 64)]), op=ALU.mult),
                      reads=[pot.b, rc[g].b], writes=[Y.sub(hc)])
            if j == 3:
                ph.op("sp", lambda e: e.dma_start(
                    out=ybuf[qc * 512:(qc + 1) * 512, 0:512].rearrange("(c p) f -> p c f", p=128), in_=Y[:, :, :]),
                    reads=[Y.sub(h * 64) for h in range(8)], dma=Y.b)

        for n in range(N + 4):
            if n < N:
                qc, j, kt = steps[n]
                for g in range(2):
                    r0 = g * 64
                    ph.op("pe", lambda e, n=n, j=j, r0=r0, kt=kt, qc=qc, g=g: e.matmul(
                        ps[n % 2][:, g * 512:(g + 1) * 512], lhsT=kT[r0:r0 + 64, kt * 128:(kt + 1) * 128],
                        rhs=qT[j][r0:r0 + 64, qc * 512:(qc + 1) * 512], start=True, stop=True),
                        reads=[kT.b, qT[j].b], writes=[ps[n % 2].b])
                ph.op("act", lambda e, n=n: e.activation(out=pb[n % 3][:], in_=ps[n % 2][:], func=AF.Exp, scale=SCALE),
                      reads=[ps[n % 2].b], writes=[pb[n % 3].b])
            m = n - 1
            if 0 <= m < N:
                qc, j, kt = steps[m]
                grp = m // NT
                a = grp % 2
                for g in range(2):
                    ph.op("pe", lambda e, m=m, g=g, kt=kt: e.matmul(
                        pov[0:65, g * 512:(g + 1) * 512], lhsT=v4[:, kt, g, :], rhs=pb[m % 3][:, g * 512:(g + 1) * 512],
                        start=(kt == 0), stop=(kt == NT - 1)),
                        reads=[pb[m % 3].b, vext.sub((kt // 8, g)), vext.sub("ones")], writes=[pov.b])
                if kt == NT - 1:
                    ph.op("dve", lambda e, a=a: e.tensor_copy(out=oT[a][0:65, :], in_=pov[0:65, :]),
                          reads=[pov.b], writes=[oT[a].b])
                    deferred.setdefault(n + 3, []).append((grp, qc, j))
            for args in deferred.pop(n, []):
                finalize_pe(*args)
        assert not deferred
        ph.close()


def dram_ap(base, row0, c0, dims):
    return bass.AP(base.tensor, base.offset + row0 * IN_W + c0, [list(d) for d in dims])


def pi_dma_specs(d):
    nb = NT // d
    specs = []
    if d == 1:
        for q4 in range(4):
            specs.append(((q4 * 8, 8, 1), q4 * 8 * 128, (1, 128)))
    elif d == 4:
        for r in range(4):
            specs.append(((r * nb, nb, 1), r, (d, 128 * d)))
    else:
        for ib in range(nb):
            specs.append(((ib, d, nb), d * 128 * ib, (d, 1)))
    return specs


def phase3(nc, l, proj, ybuf, c_mask, ident_b, ident_f):
    ph = Phase(nc, f"p3l{l}")
    with ExitStack() as es:
        T = lambda name, shape, dt, psum=False: Tile(es, nc, f"p3l{l}_{name}", shape, dt, psum)
        mask_f = T("mask_f", [128, 384], F32)
        mask = T("mask", [128, 384], BF16)
        qtm = [T(f"qtm{i}", [128, NT, 128], BF16) for i in range(2)]
        ktm = [T(f"ktm{i}", [128, NT, 128], BF16) for i in range(2)]
        qT = [T(f"qT{i}", [128, S], BF16) for i in range(2)]
        kT = [T(f"kT{i}", [128, S], BF16) for i in range(2)]
        vext = [T(f"vext{i}", [128, NT * 2 * 65], BF16) for i in range(2)]
        accT = [T(f"acc{i}", [128, S], F32) for i in range(2)]
        pb = [T(f"pb{i}", [128, 384], BF16) for i in range(3)]
        pbm = [T(f"pbm{i}", [128, 384], BF16) for i in range(3)]
        rc = [T(f"rc{i}", [128, 4], F32) for i in range(2)]
        yst = [T(f"yst{i}", [128, 4, 128], BF16) for i in range(2)]
        ptr = T("ptr", [128, 8, 128], BF16, psum=True)
        ptr2 = T("ptr2", [128, 8, 128], BF16, psum=True)
        potc = [T(f"potc{i}", [128, 4, 65], F32) for i in range(2)]
        ps = [T(f"ps{i}", [128, 512], F32, psum=True) for i in range(3)]
        pov = [T(f"pov{i}", [128, 512], F32, psum=True) for i in range(2)]
        pot = T("pot", [128, 4, 128], F32, psum=True)

        ph.op("sp", lambda e: e.dma_start(out=mask_f[:], in_=c_mask[:, :]), writes=[mask_f.b], dma=mask_f.b)
        ph.op("dve", lambda e: e.tensor_copy(out=mask[:], in_=mask_f[:]), reads=[mask_f.b], writes=[mask.b])
        v4 = [v.t[:].rearrange("p (t g c) -> p t g c", t=NT, g=2, c=65) for v in vext]
        for i in range(2):
            ph.op("pool", lambda e, i=i: e.memset(v4[i][:, :, :, 64:65], 1.0), writes=[vext[i].sub("ones")])

        units = [(j, d) for j in range(NPAIRS) for d in PATTERNS]

        def issue_loads(u):
            j, d = units[u]
            i = u % 2
            for k, (ts, row0, (rs1, rs2)) in enumerate(pi_dma_specs(d)):
                t0, cnt, tstep = ts
                tsl = slice(t0, t0 + (cnt - 1) * tstep + 1, tstep)
                for (tmt, c0) in ((qtm[i], C_QB + j * 128), (ktm[i], C_KB + j * 128)):
                    ph.op("sp", lambda e, tmt=tmt, c0=c0, tsl=tsl, row0=row0, rs1=rs1, rs2=rs2, cnt=cnt: e.dma_start(
                        out=tmt[:, tsl, :], in_=dram_ap(proj, row0, c0, [(rs1 * IN_W, 128), (rs2 * IN_W, cnt), (1, 128)])),
                        writes=[tmt.sub(k)], dma=tmt.sub(k), after=[tmt.b])
                for g in range(2):
                    c0 = C_VB + j * 128 + g * 64
                    ph.op("sp", lambda e, i=i, g=g, c0=c0, tsl=tsl, row0=row0, rs1=rs1, rs2=rs2, cnt=cnt: e.dma_start(
                        out=v4[i][:, tsl, g, 0:64], in_=dram_ap(proj, row0, c0, [(rs1 * IN_W, 128), (rs2 * IN_W, cnt), (1, 64)])),
                        writes=[vext[i].sub((k, g))], dma=vext[i].sub((k, g)), after=[vext[i].b])
            return len(pi_dma_specs(d))

        def tile_sub(d, t):
            nb = NT // d
            if d == 1:
                return t // 8
            if d == 4:
                return t // nb
            return t % nb

        def do_transposes(u):
            j, d = units[u]
            i = u % 2
            n = 0
            for (tmt, dst) in ((qtm[i], qT[i]), (ktm[i], kT[i])):
                for q4 in range(4):
                    P, eng = (ptr, "dve") if n % 2 == 0 else (ptr2, "act")
                    n += 1
                    for t in range(q4 * 8, q4 * 8 + 8):
                        ph.op("pe", lambda e, t=t, tmt=tmt, P=P: e.transpose(out=P[:, t % 8, :], in_=tmt[:, t, :], identity=ident_b[:]),
                              reads=[tmt.sub(tile_sub(d, t)), tmt.b, ident_b.b], writes=[P.b])
                    if eng == "dve":
                        ph.op("dve", lambda e, q4=q4, dst=dst, P=P: e.tensor_copy(out=dst[:, q4 * 1024:(q4 + 1) * 1024],
                                                                               in_=P[:].rearrange("p a b -> p (a b)")),
                              reads=[P.b], writes=[dst.sub(q4)])
                    else:
                        ph.op("act", lambda e, q4=q4, dst=dst, P=P: e.copy(out=dst[:, q4 * 1024:(q4 + 1) * 1024],
                                                                        in_=P[:].rearrange("p a b -> p (a b)")),
                              reads=[P.b], writes=[dst.sub(q4)])

        def finalize_pair(j):
            for qc in range(S // 512):
                Y = yst[qc % 2]
                for g in range(2):
                    for c in range(4):
                        ph.op("pe", lambda e, c=c, g=g, qc=qc: e.transpose(
                            out=pot[:, c, 0:65], in_=accT[g][0:65, qc * 512 + c * 128:qc * 512 + (c + 1) * 128],
                            identity=ident_f[0:65, 0:65]), reads=[accT[g].b, ident_f.b], writes=[pot.b])
                    a = g
                    PC = potc[g]
                    ph.op("dve", lambda e, PC=PC: e.tensor_copy(out=PC[:, :, :], in_=pot[:, :, 0:65]), reads=[pot.b], writes=[PC.b])
                    ph.op("dve", lambda e, a=a, PC=PC: e.reciprocal(out=rc[a][:], in_=PC[:, :, 64]), reads=[PC.b], writes=[rc[a].b])
                    ph.op("pool", lambda e, a=a, Y=Y, g=g, PC=PC: e.tensor_tensor(out=Y[:, :, g * 64:(g + 1) * 64], in0=PC[:, :, 0:64],
                                                                                 in1=fap(rc[a][:], [(1, 4), (0, 64)]), op=ALU.mult),
                          reads=[PC.b, rc[a].b], writes=[Y.sub(g)])
                ph.op("sp", lambda e, Y=Y, qc=qc, j=j: e.dma_start(
                    out=ybuf[qc * 512:(qc + 1) * 512, 512 + j * 128:512 + (j + 1) * 128].rearrange("(c p) f -> p c f", p=128),
                    in_=Y[:, :, :]), reads=[Y.sub(0), Y.sub(1)], dma=Y.b)

        issue_loads(0)
        cnt = 0
        for u, (j, d) in enumerate(units):
            i = u % 2
            nb = NT // d
            if u + 1 < len(units):
                issue_loads(u + 1)
            do_transposes(u)
            steps = [(g, tq) for g in range(2) for tq in range(NT)]
            N = len(steps)
            info = {}
            LAG = 2
            for n in range(N + LAG):
                if n < N:
                    g, tq = steps[n]
                    r0 = g * 64
                    ib = tq % nb
                    tks = [(tq + o, o + 1) for o in (-1, 0, 1) if 0 <= ib + o < nb]
                    k = cnt % 3
                    cnt += 1
                    for c, (tk, mi) in enumerate(tks):
                        ph.op("pe", lambda e, k=k, c=c, tk=tk, tq=tq, r0=r0, i=i: e.matmul(
                            ps[k][:, c * 128:(c + 1) * 128], lhsT=kT[i][r0:r0 + 64, tk * 128:(tk + 1) * 128],
                            rhs=qT[i][r0:r0 + 64, tq * 128:(tq + 1) * 128], start=True, stop=True),
                            reads=[kT[i].sub(q) for q in range(4)] + [qT[i].sub(q) for q in range(4)], writes=[ps[k].b])
                    w = len(tks) * 128
                    m0 = tks[0][1] * 128
                    ph.op("act", lambda e, k=k, w=w: e.activation(out=pb[k][:, 0:w], in_=ps[k][:, 0:w], func=AF.Exp, scale=SCALE),
                          reads=[ps[k].b], writes=[pb[k].b])
                    ph.op("pool" if cnt % 3 == 0 else "dve", lambda e, k=k, w=w, m0=m0: e.tensor_tensor(out=pbm[k][:, 0:w], in0=pb[k][:, 0:w],
                                                                             in1=mask[:, m0:m0 + w], op=ALU.mult),
                          reads=[pb[k].b, mask.b], writes=[pbm[k].b])
                    info[n] = (k, tks)
                m = n - LAG
                if 0 <= m < N:
                    g, tq = steps[m]
                    k, tks = info[m]
                    a = m % 2
                    for c, (tk, mi) in enumerate(tks):
                        ph.op("pe", lambda e, a=a, c=c, tk=tk, g=g, k=k, i=i, nk=len(tks): e.matmul(
                            pov[a][0:65, 0:128], lhsT=v4[i][:, tk, g, :], rhs=pbm[k][:, c * 128:(c + 1) * 128],
                            start=(c == 0), stop=(c == nk - 1)),
                            reads=[pbm[k].b, vext[i].sub((tile_sub(d, tk), g)), vext[i].sub("ones"), vext[i].b], writes=[pov[a].b])
                    r, ib = tq // nb, tq % nb
                    col0 = r + d * 128 * ib
                    av = fap(accT[g][0:65, col0:col0 + 1], [(d, 128)])
                    if d == PATTERNS[0]:
                        ph.op("dve", lambda e, av=av, a=a: e.tensor_copy(out=av, in_=pov[a][0:65, 0:128]),
                              reads=[pov[a].b], writes=[accT[g].b])
                    else:
                        ph.op("dve", lambda e, av=av, a=a: e.tensor_tensor(out=av, in0=av, in1=pov[a][0:65, 0:128], op=ALU.add),
                              reads=[pov[a].b, accT[g].b], writes=[accT[g].b])
            if d == PATTERNS[-1]:
                finalize_pair(j)
        ph.close()


def phase4(nc, l, x_src, ybuf, w_out, w_mi, w_mo, g_oa, g_ob, g_n2, g_fin, dst, last, ident_b, nt, wi_pre=None):
    ph = Phase(nc, f"p4l{l}")
    with ExitStack() as es:
        T = lambda name, shape, dt, psum=False: Tile(es, nc, f"p4l{l}_{name}", shape, dt, psum)
        wo = [T(f"wo{k}", [128, D], BF16) for k in range(8)]
        wi = wi_pre if wi_pre is not None else [T(f"wi{k}", [128, DFF], BF16) for k in range(8)]
        wo2 = [T(f"wo2{k}", [128, 4, D], BF16) for k in range(8)]
        goa = T("goa", [128, 512], F32)
        gob = T("gob", [128, 512], F32)
        g2 = T("g2", [128, D], F32)
        gfin = T("gfin", [128, D], F32) if last else None
        NB = 2
        y_t = [T(f"y{i}", [128, D], BF16) for i in range(3)]
        x_t = [T(f"x{i}", [128, D], F32) for i in range(3)]
        yn = [T(f"yn{i}", [128, D], BF16) for i in range(NB)]
        yT = [T(f"yT{i}", [128, 8, 128], BF16) for i in range(NB)]
        h2 = [T(f"h2{i}", [128, D], BF16) for i in range(NB)]
        h2T = [T(f"h2T{i}", [128, 8, 128], BF16) for i in range(NB)]
        st = [{k: T(f"{k}{i}", [128, 2], F32) for k in ("ssy", "sdy", "rsy", "ss2", "sd2", "rs2", "ssf", "sdf", "rsf")} for i in range(NB)]
        junk = T("junk", [128, D], BF16)
        r_sb = [T(f"r{i}", [128, 512], F32) for i in range(2)]
        u_bf = [T(f"u{i}", [128, 512], BF16) for i in range(2)]
        uT = T("uT", [128, 32, 128], BF16)
        pT = [T(f"pT{i}", [128, 8, 128], BF16, psum=True) for i in range(2)]
        pw = [T(f"pw{i}", [128, 512], F32, psum=True) for i in range(2)]
        pu = [T(f"pu{i}", [128, 512], F32, psum=True) for i in range(2)]
        po2 = [T(f"po2{i}", [128, 512], F32, psum=True) for i in range(2)]

        wov = w_out[l].rearrange("(kc p) n -> kc p n", p=128)
        wiv = w_mi[l].rearrange("(kc p) n -> kc p n", p=128)
        wo2v = w_mo[l].rearrange("(k4 kc p) n -> k4 p kc n", p=128, kc=4)
        for k in range(8):
            ph.op("pool", lambda e, k=k: e.dma_start(out=wo[k][:], in_=wov[k]), writes=[wo[k].b], dma=wo[k].b)
        if wi_pre is None:
            for k in range(8):
                ph.op("pool", lambda e, k=k: e.dma_start(out=wi[k][:], in_=wiv[k]), writes=[wi[k].b], dma=wi[k].b)
        for k in range(8):
            ph.op("pool", lambda e, k=k: e.dma_start(out=wo2[k][:], in_=wo2v[k]), writes=[wo2[k].b], dma=wo2[k].b)
        ph.op("sp", lambda e: e.dma_start(out=goa[:], in_=g_oa[l:l + 1, :].partition_broadcast(128)), writes=[goa.b], dma=goa.b)
        ph.op("sp", lambda e: e.dma_start(out=gob[:], in_=g_ob[l:l + 1, :].partition_broadcast(128)), writes=[gob.b], dma=gob.b)
        ph.op("sp", lambda e: e.dma_start(out=g2[:], in_=g_n2[l:l + 1, :].partition_broadcast(128)), writes=[g2.b], dma=g2.b)
        if last:
            ph.op("sp", lambda e: e.dma_start(out=gfin[:], in_=g_fin[0:1, :].partition_broadcast(128)), writes=[gfin.b], dma=gfin.b)

        def norm_scale(src, ss, sd, rs, dst_t, gain, width, col0, ncols):
            for j in range(ncols):
                c0 = col0 + j * width
                ph.op("act", lambda e, c0=c0, j=j: e.activation(out=junk[:, c0:c0 + width], in_=src[:, c0:c0 + width],
                                                              func=AF.Square, accum_out=ss[:, j:j + 1]),
                      reads=[src.b], writes=[ss.b])
            rstd_ops(ph, ss, sd, rs, 1.0 / width, ncols)
            for j in range(ncols):
                c0 = col0 + j * width
                ph.op("dve", lambda e, c0=c0, j=j: e.scalar_tensor_tensor(
                    out=dst_t[:, c0:c0 + width], in0=src[:, c0:c0 + width], scalar=rs[:, j:j + 1], in1=gain[j][:, 0:width],
                    op0=ALU.mult, op1=ALU.mult), reads=[src.b, rs.b, gain[j].b], writes=[dst_t.b])

        def A0_load(t):
            ph.op("sp", lambda e: e.dma_start(out=y_t[t % 3][:], in_=ybuf[t * 128:(t + 1) * 128, :]), writes=[y_t[t % 3].b], dma=y_t[t % 3].b)
            ph.op("sp", lambda e: e.dma_start(out=x_t[t % 3][:], in_=x_src[t * 128:(t + 1) * 128, :]), writes=[x_t[t % 3].b], dma=x_t[t % 3].b)

        def A0_norm(t):
            i = t % NB
            norm_scale(y_t[t % 3], st[i]["ssy"], st[i]["sdy"], st[i]["rsy"], yn[i], [goa, gob], 512, 0, 2)

        def A1t(t, bank):
            i = t % NB
            transpose8(ph, yn[i], yT[i], pT[bank], ident_b, "act")

        def A1m(t):
            i = t % NB
            for nh in range(2):
                for kc in range(8):
                    ph.op("pe", lambda e, nh=nh, kc=kc: e.matmul(pw[nh][:, :], lhsT=yT[i][:, kc, :], rhs=wo[kc][:, nh * 512:(nh + 1) * 512],
                                                               start=(kc == 0), stop=(kc == 7)),
                          reads=[yT[i].b, wo[kc].b], writes=[pw[nh].b])
            for nh in range(2):
                ph.op("dve", lambda e, nh=nh: e.tensor_tensor(out=x_t[t % 3][:, nh * 512:(nh + 1) * 512], in0=x_t[t % 3][:, nh * 512:(nh + 1) * 512],
                                                             in1=pw[nh][:, :], op=ALU.add),
                      reads=[x_t[t % 3].b, pw[nh].b], writes=[x_t[t % 3].b])
            if DEBUG_XM is not None:
                ph.op("sp", lambda e: e.dma_start(out=DEBUG_XM[t * 128:(t + 1) * 128, :], in_=x_t[t % 3][:]), reads=[x_t[t % 3].b], dma=x_t[t % 3].sub("dbg"))
            norm_scale(x_t[t % 3], st[i]["ss2"], st[i]["sd2"], st[i]["rs2"], h2[i], [g2], D, 0, 1)

        def A2(t):
            i = t % NB
            transpose8(ph, h2[i], h2T[i], pT[1], ident_b, "act")

        def u_transposes(c):
            x = (c // 2) % 2
            for q in range(4):
                ph.op("pe", lambda e, q=q: e.transpose(out=pT[x][:, (c % 2) * 4 + q, :], in_=u_bf[c % 2][:, q * 128:(q + 1) * 128],
                                                      identity=ident_b[:]),
                      reads=[u_bf[c % 2].b, ident_b.b], writes=[pT[x].b])
            if c % 2 == 1:
                ph.op("dve", lambda e: e.tensor_copy(out=uT[:, (c - 1) * 4:(c + 1) * 4, :], in_=pT[x][:]),
                      reads=[pT[x].b], writes=[uT.sub(c // 2)])

        def B(t, upto=8, hook=None):
            i = t % NB
            for c in range(upto):
                for kc in range(8):
                    ph.op("pe", lambda e, c=c, kc=kc: e.matmul(pu[c % 2][:, :], lhsT=h2T[i][:, kc, :], rhs=wi[kc][:, c * 512:(c + 1) * 512],
                                                             start=(kc == 0), stop=(kc == 7)),
                          reads=[h2T[i].b, wi[kc].b], writes=[pu[c % 2].b])
                ph.op("act", lambda e, c=c: e.activation(out=r_sb[c % 2][:], in_=pu[c % 2][:], func=AF.Relu),
                      reads=[pu[c % 2].b], writes=[r_sb[c % 2].b])
                ph.op("dve", lambda e, c=c: e.tensor_tensor(out=u_bf[c % 2][:], in0=r_sb[c % 2][:], in1=r_sb[c % 2][:], op=ALU.mult),
                      reads=[r_sb[c % 2].b], writes=[u_bf[c % 2].b])
                if c >= 1:
                    u_transposes(c - 1)
                if c == 6 and hook is not None:
                    hook()

        def C_mm(t):
            i = t % NB
            for nh in range(2):
                for kc in range(32):
                    ph.op("pe", lambda e, nh=nh, kc=kc: e.matmul(po2[nh][:, :], lhsT=uT[:, kc, :],
                                                               rhs=wo2[kc // 4][:, kc % 4, nh * 512:(nh + 1) * 512],
                                                               start=(kc == 0), stop=(kc == 31)),
                          reads=[uT.sub(kc // 8), wo2[kc // 4].b], writes=[po2[nh].b])
            for nh in range(2):
                ph.op("dve", lambda e, nh=nh: e.tensor_tensor(out=x_t[t % 3][:, nh * 512:(nh + 1) * 512], in0=x_t[t % 3][:, nh * 512:(nh + 1) * 512],
                                                             in1=po2[nh][:, :], op=ALU.add),
                      reads=[x_t[t % 3].b, po2[nh].b], writes=[x_t[t % 3].b])

        def C_fin(t):
            i = t % NB
            if last:
                norm_scale(x_t[t % 3], st[i]["ssf"], st[i]["sdf"], st[i]["rsf"], x_t[t % 3], [gfin], D, 0, 1)
            ph.op("sp", lambda e: e.dma_start(out=dst[t * 128:(t + 1) * 128, :], in_=x_t[t % 3][:]), reads=[x_t[t % 3].b], dma=x_t[t % 3].b)

        if SIMPLE_P4:
            for t in range(nt):
                A0_load(t)
                A0_norm(t)
                A1t(t, 0)
                A1m(t)
                A2(t)
                B(t)
                u_transposes(7)
                C_mm(t)
                C_fin(t)
        else:
            A0_load(0)
            if nt > 1:
                A0_load(1)
            A0_norm(0)
            A1t(0, 0)
            A1m(0)
            A2(0)
            for t in range(nt):
                if t + 2 < nt:
                    A0_load(t + 2)
                if t + 1 < nt:
                    A0_norm(t + 1)
                B(t, hook=(lambda t=t: A1t(t + 1, 1)) if t + 1 < nt else None)
                if t + 1 < nt:
                    A1m(t + 1)
                u_transposes(7)
                C_mm(t)
                if t + 1 < nt:
                    A2(t + 1)
                C_fin(t)
        ph.close()


def _perm_in():
    qa = []
    for j in range(4):
        for h in (j, j + 4):
            qa += list(range(h * 64, h * 64 + 64))
    return np.array(qa + list(range(512, IN_W)), dtype=np.int64)


def _consts():
    ident = np.eye(128, dtype=np.float32)
    t = np.arange(S)
    half = 16
    fr = (10000.0 ** (-np.arange(half, dtype=np.float32) / half)).astype(np.float32)
    ang_r = (t // 64).astype(np.float32)[:, None] * fr[None, :]
    ang_c = (t % 64).astype(np.float32)[:, None] * fr[None, :]
    cr, sr, cc, sc = np.cos(ang_r), np.sin(ang_r), np.cos(ang_c), np.sin(ang_c)
    ropeA = np.concatenate([cr, cr, cc, cc, -sr, sr, -sc, sc], axis=1).astype(np.float32)
    fb = (500000.0 ** (-np.arange(8, dtype=np.float32) / 8)).astype(np.float32)
    ang = t.astype(np.float32)[:, None] * fb[None, :]
    cb, sb = np.cos(ang), np.sin(ang)
    ropeB = np.concatenate([cb, cb, -sb, sb], axis=1).astype(np.float32)
    kk = np.arange(128)[:, None]
    qq = np.arange(128)[None, :]
    mask = np.concatenate([(np.abs(o + kk - qq) <= 64).astype(np.float32) for o in (-128, 0, 128)], axis=1)
    return ident, ropeA, ropeB, mask


def make_in_maps(inputs, n_cores=8):
    perm = _perm_in()
    ident, ropeA, ropeB, mask = _consts()
    rowperm = np.concatenate([perm[:512], np.arange(512, 1024)])
    w_in = np.ascontiguousarray(np.asarray(inputs["w_in"], dtype=np.float32)[:, :, perm])
    shared = {
        "w_in": w_in,
        "w_out": np.ascontiguousarray(np.asarray(inputs["w_out"], dtype=np.float32)[:, rowperm, :]),
        "w_mlp_in": np.ascontiguousarray(inputs["w_mlp_in"], dtype=np.float32),
        "w_mlp_out": np.ascontiguousarray(inputs["w_mlp_out"], dtype=np.float32),
        "norm1": np.ascontiguousarray(inputs["norm1"], dtype=np.float32),
        "norm2": np.ascontiguousarray(inputs["norm2"], dtype=np.float32),
        "q_norm": np.ascontiguousarray(inputs["q_norm"], dtype=np.float32),
        "k_norm": np.ascontiguousarray(inputs["k_norm"], dtype=np.float32),
        "out_norm_a": np.ascontiguousarray(np.asarray(inputs["out_norm_a"], dtype=np.float32)[:, perm[:512]]),
        "out_norm_b": np.ascontiguousarray(inputs["out_norm_b"], dtype=np.float32),
        "final_norm": np.ascontiguousarray(np.asarray(inputs["final_norm"], dtype=np.float32).reshape(1, D)),
        "c_ident": ident, "c_ropeA": ropeA, "c_ropeB": ropeB, "c_mask": mask,
    }
    x = np.asarray(inputs["x"], dtype=np.float32)
    maps = []
    for c in range(n_cores):
        m = dict(shared)
        m["x"] = np.ascontiguousarray(x[(c // 2) % x.shape[0]])
        maps.append(m)
    return maps


def kernel(**inputs):
    nc = build_program()
    in_maps = make_in_maps(inputs, 8)
    res = run_bass_kernel_spmd(nc, in_maps, core_ids=list(range(8)))
    outs = [np.asarray(res.results[2 * b]["out"], dtype=np.float32) for b in range(4)]
    return np.stack(outs, axis=0)
```

```python
import numpy as np
from contextlib import ExitStack
import concourse.bass as bass
import concourse.mybir as mybir
from concourse.bass_utils import run_bass_kernel_spmd

F32 = mybir.dt.float32
BF16 = mybir.dt.bfloat16
AF = mybir.ActivationFunctionType
ALU = mybir.AluOpType
AX = mybir.AxisListType

S = 4096
D = 1024
NT = S // 128
HD = 64
DEPTH = 4
IN_W = 2304
DFF = 4096
EPS = 1e-6
SCALE = HD ** -0.5
SELF_SYNC = True
PATTERNS = (1, 4, 16)
NPAIRS = 4
DEBUG_XM = None
DUMP = None
SIMPLE_P4 = False

C_QA, C_KA, C_VA, C_QB, C_KB, C_VB = 0, 512, 640, 768, 1280, 1792


def fap(ap, dims):
    return bass.AP(ap.tensor, ap.offset, [list(ap.ap[0])] + [list(d) for d in dims])


class _Sem:
    __slots__ = ("h", "total", "owner")

    def __init__(self, h, owner=None):
        self.h = h
        self.total = 0
        self.owner = owner


class Buf:
    __slots__ = ("name", "w", "r", "dsem")

    def __init__(self, name):
        self.name = name
        self.w = None
        self.r = {}
        self.dsem = None


class SemPool:
    def __init__(self, es, nc, n):
        self.free = [es.enter_context(nc.semaphore(f"sp{i}")) for i in range(n)]

    def get(self):
        return self.free.pop(0)

    def put(self, hs):
        self.free = list(hs) + self.free


POOL = None
SWSEMS = []


class Phase:
    ENGS = (("pe", "tensor"), ("act", "scalar"), ("dve", "vector"), ("pool", "gpsimd"), ("sp", "sync"))

    def __init__(self, nc, name):
        self.nc = nc
        self.name = name
        self.es = ExitStack()
        self.q = {e: [] for e, _ in self.ENGS}
        self.waited = {e: {} for e, _ in self.ENGS}
        self.esem = {}
        for e in ("pe", "act", "dve", "pool"):
            self.esem[e] = _Sem(POOL.get(), self)
        self.dsems = []
        self.swsems = []
        self.nsw = 0

    def _dma_sem(self, buf, sw=False):
        if buf.dsem is None or buf.dsem[0] is not self:
            if sw:
                s = SWSEMS[self.nsw]
                self.nsw += 1
                s.owner = self
                self.swsems.append(s)
            else:
                s = _Sem(POOL.get(), self)
                self.dsems.append(s)
            buf.dsem = (self, s)
        return buf.dsem[1]

    def op(self, eng, fn, reads=(), writes=(), dma=None, after=()):
        deps = []
        for b in after:
            if b.w is not None:
                deps.append(b.w)
            deps.extend(b.r.values())
        for b in reads:
            if b.w is not None:
                deps.append(b.w)
        for b in writes:
            if b.w is not None:
                deps.append(b.w)
            deps.extend(b.r.values())
        waits = {}
        for (s, v, src, phs) in deps:
            if phs is not self:
                continue
            if src == eng and (eng == "pe" or not SELF_SYNC):
                continue
            if self.waited[eng].get(s, 0) >= v:
                continue
            if src == "dma":
                assert v == s.total, f"DMA semaphore reuse hazard in {self.name}"
            if waits.get(s, 0) < v:
                waits[s] = v
        for s, v in waits.items():
            self.waited[eng][s] = v
        if dma is not None:
            sem = self._dma_sem(dma, sw=(eng == "pool"))
            sem.total += 16
            ev = (sem, sem.total, "dma", self)
            amt = 16
        else:
            sem = self.esem[eng]
            sem.total += 1
            ev = (sem, sem.total, eng, self)
            amt = 1
        self.q[eng].append((list(waits.items()), fn, sem, amt))
        for b in reads:
            old = b.r.get(ev[0])
            if old is None or old[3] is not self or old[1] < ev[1]:
                b.r[ev[0]] = ev
        for b in writes:
            b.w = ev
            b.r = {}
        return ev

    def close(self):
        fin = [(s, s.total) for s in self.dsems + self.swsems if s.total > 0 and self.waited["sp"].get(s, 0) < s.total]
        self.q["sp"].append((fin, None, None, 0))
        allsems = list(self.esem.values()) + self.dsems
        with self.nc.Block() as cb:
            def fclear(e):
                for s in allsems:
                    e.sem_clear(s.h)
            cb.gpsimd(fclear)
        with self.nc.Block() as block:
            for eng, attr in self.ENGS:
                items = self.q[eng]
                if not items:
                    continue

                def f(e, items=items):
                    for waits, fn, sem, amt in items:
                        for (s, v) in waits:
                            e.wait_ge(s.h, v)
                        if fn is not None:
                            fn(e).then_inc(sem.h, amt)
                getattr(block, attr)(f)
        POOL.put([s.h for s in allsems])
        self.es.close()


class Tile:
    def __init__(self, es, nc, name, shape, dtype, psum=False):
        if psum:
            self.t = es.enter_context(nc.psum_tensor(name, shape, dtype))
        else:
            self.t = es.enter_context(nc.sbuf_tensor(name, shape, dtype))
        self.b = Buf(name)

        self.subs = {}

    def sub(self, key):
        if key not in self.subs:
            self.subs[key] = Buf(f"{self.b.name}_{key}")
        return self.subs[key]

    def __getitem__(self, k):
        return self.t[k]


def build_program(n_layers=DEPTH, debug=False, nt1=NT, phases=(1, 2, 3, 4)):
    nc = bass.Bass("TRN2", target_bir_lowering=False)
    dk = "ExternalOutput" if debug else "Internal"
    x_in = nc.dram_tensor("x", [S, D], F32, kind="ExternalInput").ap()
    w_in = nc.dram_tensor("w_in", [DEPTH, D, IN_W], F32, kind="ExternalInput").ap()
    w_out = nc.dram_tensor("w_out", [DEPTH, D, D], F32, kind="ExternalInput").ap()
    w_mi = nc.dram_tensor("w_mlp_in", [DEPTH, D, DFF], F32, kind="ExternalInput").ap()
    w_mo = nc.dram_tensor("w_mlp_out", [DEPTH, DFF, D], F32, kind="ExternalInput").ap()
    g_n1 = nc.dram_tensor("norm1", [DEPTH, D], F32, kind="ExternalInput").ap()
    g_n2 = nc.dram_tensor("norm2", [DEPTH, D], F32, kind="ExternalInput").ap()
    g_q = nc.dram_tensor("q_norm", [DEPTH, HD], F32, kind="ExternalInput").ap()
    g_k = nc.dram_tensor("k_norm", [DEPTH, HD], F32, kind="ExternalInput").ap()
    g_oa = nc.dram_tensor("out_norm_a", [DEPTH, 512], F32, kind="ExternalInput").ap()
    g_ob = nc.dram_tensor("out_norm_b", [DEPTH, 512], F32, kind="ExternalInput").ap()
    g_fin = nc.dram_tensor("final_norm", [1, D], F32, kind="ExternalInput").ap()
    c_ident = nc.dram_tensor("c_ident", [128, 128], F32, kind="ExternalInput").ap()
    c_ropeA = nc.dram_tensor("c_ropeA", [S, 128], F32, kind="ExternalInput").ap()
    c_ropeB = nc.dram_tensor("c_ropeB", [S, 32], F32, kind="ExternalInput").ap()
    c_mask = nc.dram_tensor("c_mask", [128, 384], F32, kind="ExternalInput").ap()
    out = nc.dram_tensor("out", [S, D], F32, kind="ExternalOutput").ap()
    proj = nc.dram_tensor("proj", [S, IN_W], BF16, kind=dk).ap()
    ybuf = nc.dram_tensor("ybuf", [S, D], BF16, kind=dk).ap()
    xres = nc.dram_tensor("xres", [S, D], F32, kind=dk).ap()

    global DEBUG_XM
    DEBUG_XM = xres if (debug and n_layers == 1) else None
    global POOL, SWSEMS
    with ExitStack() as ges:
        POOL = SemPool(ges, nc, 64)
        SWSEMS[:] = [_Sem(ges.enter_context(nc.semaphore(f"sw{i}"))) for i in range(24)]
        ident_f = Tile(ges, nc, "ident_f", [128, 128], F32)
        ident_b = Tile(ges, nc, "ident_b", [128, 128], BF16)
        wi_g = [Tile(ges, nc, f"wi_g{k}", [128, DFF], BF16) for k in range(8)]
        ph = Phase(nc, "c0")
        ph.op("sp", lambda e: e.dma_start(out=ident_f[:], in_=c_ident[:, :]), writes=[ident_f.b], dma=ident_f.b)
        ph.op("pool", lambda e: e.dma_start(out=ident_b[:], in_=c_ident[:, :]), writes=[ident_b.b], dma=ident_b.b)
        ph.close()

        for l in range(n_layers):
            x_src = x_in if l == 0 else xres
            if 1 in phases:
                phase1(nc, l, x_src, w_in, g_n1, g_q, g_k, c_ropeA, c_ropeB, proj, ident_b, nt1)
            if 2 in phases:
                phase2(nc, l, proj, ybuf, ident_b, ident_f, wi_g if 4 in phases else None, w_mi)
            if 3 in phases:
                phase3(nc, l, proj, ybuf, c_mask, ident_b, ident_f)
            if 4 in phases:
                last = (l == n_layers - 1)
                phase4(nc, l, x_src, ybuf, w_out, w_mi, w_mo, g_oa, g_ob, g_n2, g_fin,
                       out if last else xres, last, ident_b, nt1, wi_g if 2 in phases else None)
    return nc


def rstd_ops(ph, ss, tmp, rs, scale, n=None):
    sl = slice(None) if n is None else slice(0, n)
    ph.op("dve", lambda e: e.tensor_scalar(out=tmp[:, sl], in0=ss[:, sl], scalar1=scale, scalar2=EPS, op0=ALU.mult, op1=ALU.add),
          reads=[ss.b], writes=[tmp.b])
    ph.op("act", lambda e: e.activation(out=tmp[:, sl], in_=tmp[:, sl], func=AF.Sqrt), reads=[tmp.b], writes=[tmp.b])
    ph.op("dve", lambda e: e.reciprocal(out=rs[:, sl], in_=tmp[:, sl]), reads=[tmp.b], writes=[rs.b])


def transpose8(ph, src, dst, pT, ident_b, evac_eng):
    for kc in range(8):
        ph.op("pe", lambda e, kc=kc: e.transpose(out=pT[:, kc, :], in_=src[:, kc * 128:(kc + 1) * 128], identity=ident_b[:]),
              reads=[src.b, ident_b.b], writes=[pT.b])
    if evac_eng == "act":
        ph.op("act", lambda e: e.copy(out=dst[:], in_=pT[:]), reads=[pT.b], writes=[dst.b])
    else:
        ph.op("dve", lambda e: e.tensor_copy(out=dst[:], in_=pT[:]), reads=[pT.b], writes=[dst.b])


def phase1(nc, l, x_src, w_in, g_n1, g_q, g_k, c_ropeA, c_ropeB, proj, ident_b, nt):
    ph = Phase(nc, f"p1l{l}")
    with ExitStack() as es:
        T = lambda name, shape, dt, psum=False: Tile(es, nc, f"p1l{l}_{name}", shape, dt, psum)
        wk = [T(f"w{kc}", [128, IN_W], BF16) for kc in range(8)]
        g1 = T("g1", [128, D], F32)
        gq = T("gq", [128, HD], F32)
        gk = T("gk", [128, HD], F32)
        ropeA = T("ropeA", [128, NT, 128], F32)
        ropeB = T("ropeB", [128, NT, 32], F32)
        NB = 2
        x_t = [T(f"x{i}", [128, D], F32) for i in range(3)]
        junk = [T(f"junk{i}", [128, D], BF16) for i in range(NB)]
        ss = [T(f"ss{i}", [128, 1], F32) for i in range(NB)]
        sd = [T(f"sd{i}", [128, 1], F32) for i in range(NB)]
        rs = [T(f"rs{i}", [128, 1], F32) for i in range(NB)]
        h_b = [T(f"h{i}", [128, D], BF16) for i in range(NB)]
        hT = [T(f"hT{i}", [128, 8, 128], BF16) for i in range(NB)]
        po = [T(f"po{i}", [128, IN_W], BF16) for i in range(NB)]
        sq = [T(f"sq{i}", [128, 640], F32) for i in range(NB)]
        s8 = [T(f"s8{i}", [128, 10], F32) for i in range(NB)]
        d8 = [T(f"d8{i}", [128, 10], F32) for i in range(NB)]
        r8 = [T(f"r8{i}", [128, 10], F32) for i in range(NB)]
        qn = [T(f"qn{i}", [128, 640], F32) for i in range(NB)]
        t1 = [T(f"t1{i}", [128, 640], F32) for i in range(NB)]
        t2 = [T(f"t2{i}", [128, 640], F32) for i in range(NB)]
        u1 = [T(f"u1{i}", [128, 256], F32) for i in range(NB)]
        u2 = [T(f"u2{i}", [128, 256], F32) for i in range(NB)]
        f0 = [T(f"f0{i}", [128, 640], F32) for i in range(NB)]
        f1 = f0
        fb = [T(f"fb{i}", [128, 256], F32) for i in range(NB)]
        pT = [T(f"pT{i}", [128, 8, 128], BF16, psum=True) for i in range(1)]
        pp = [T(f"pp{i}", [128, 512], F32, psum=True) for i in range(5)]

        wv = w_in[l].rearrange("(kc p) n -> kc p n", p=128)
        for kc in range(8):
            ph.op("pool", lambda e, kc=kc: e.dma_start(out=wk[kc][:], in_=wv[kc]), writes=[wk[kc].b], dma=wk[kc].b)
        ph.op("sp", lambda e: e.dma_start(out=g1[:], in_=g_n1[l:l + 1, :].partition_broadcast(128)), writes=[g1.b], dma=g1.b)
        ph.op("sp", lambda e: e.dma_start(out=gq[:], in_=g_q[l:l + 1, :].partition_broadcast(128)), writes=[gq.b], dma=gq.b)
        ph.op("sp", lambda e: e.dma_start(out=gk[:], in_=g_k[l:l + 1, :].partition_broadcast(128)), writes=[gk.b], dma=gk.b)
        ph.op("sp", lambda e: e.dma_start(out=ropeA[:], in_=c_ropeA.rearrange("(t p) c -> p t c", p=128)), writes=[ropeA.b], dma=ropeA.b)
        ph.op("sp", lambda e: e.dma_start(out=ropeB[:], in_=c_ropeB.rearrange("(t p) c -> p t c", p=128)), writes=[ropeB.b], dma=ropeB.b)

        groups = [(C_QA, 512), (C_KA, 256), (C_QB, 512), (C_KB, 512), (C_VB, 512)]

        def slot(t):
            i = t % NB
            return (x_t[t % 3], junk[i], ss[i], sd[i], rs[i], h_b[i], hT[i], po[i], sq[i], s8[i], d8[i], r8[i], qn[i], t1[i], t2[i], u1[i], u2[i])

        def load_x(t):
            X = x_t[t % 3]
            ph.op("sp", lambda e, X=X, t=t: e.dma_start(out=X[:], in_=x_src[t * 128:(t + 1) * 128, :]), writes=[X.b], dma=X.b)

        def stage_N(t):
            X, J, SS, SD, RS, H, HT, PO, SQ, S8, D8, R8, QN, T1, T2, U1, U2 = slot(t)
            ph.op("act", lambda e, X=X, J=J, SS=SS: e.activation(out=J[:], in_=X[:], func=AF.Square, accum_out=SS[:]),
                  reads=[X.b], writes=[J.b, SS.b])
            rstd_ops(ph, SS, SD, RS, 1.0 / D)
            ph.op("dve", lambda e, X=X, RS=RS, H=H: e.scalar_tensor_tensor(out=H[:], in0=X[:], scalar=RS[:, 0:1], in1=g1[:],
                                                                         op0=ALU.mult, op1=ALU.mult),
                  reads=[X.b, RS.b, g1.b], writes=[H.b])

        def stage_T(t):
            X, J, SS, SD, RS, H, HT, PO, SQ, S8, D8, R8, QN, T1, T2, U1, U2 = slot(t)
            transpose8(ph, H, HT, pT[0], ident_b, "dve")

        def stage_M(t):
            X, J, SS, SD, RS, H, HT, PO, SQ, S8, D8, R8, QN, T1, T2, U1, U2 = slot(t)
            for gi, (c0, wd) in enumerate(groups):
                for kc in range(8):
                    ph.op("pe", lambda e, gi=gi, c0=c0, wd=wd, kc=kc, HT=HT: e.matmul(
                        pp[gi][:, 0:wd], lhsT=HT[:, kc, :], rhs=wk[kc][:, c0:c0 + wd], start=(kc == 0), stop=(kc == 7)),
                        reads=[HT.b, wk[kc].b], writes=[pp[gi].b])

        def stage_Pe_a(t):
            X, J, SS, SD, RS, H, HT, PO, SQ, S8, D8, R8, QN, T1, T2, U1, U2 = slot(t)
            i = t % NB
            F0, F1, FB = f0[i], f1[i], fb[i]
            ph.op("act", lambda e: e.copy(out=F0[:, 0:512], in_=pp[0][:, 0:512]), reads=[pp[0].b], writes=[F0.b])
            ph.op("dve", lambda e: e.tensor_copy(out=F0[:, 512:640], in_=pp[1][:, 0:128]), reads=[pp[1].b], writes=[F1.b])
            ph.op("dve", lambda e: e.tensor_copy(out=PO[:, C_VA:C_VA + 128], in_=pp[1][:, 128:256]), reads=[pp[1].b], writes=[PO.sub('va')])
            ph.op("act", lambda e: e.copy(out=PO[:, C_QB:C_QB + 512], in_=pp[2][:, :]), reads=[pp[2].b], writes=[PO.sub(2)])
            ph.op("act", lambda e: e.copy(out=PO[:, C_KB:C_KB + 512], in_=pp[3][:, :]), reads=[pp[3].b], writes=[PO.sub(3)])
            ph.op("act", lambda e: e.copy(out=PO[:, C_VB:C_VB + 512], in_=pp[4][:, :]), reads=[pp[4].b], writes=[PO.sub('vb')])

        def stage_Pe_b(t):
            X, J, SS, SD, RS, H, HT, PO, SQ, S8, D8, R8, QN, T1, T2, U1, U2 = slot(t)
            i = t % NB
            F0, F1, FB = f0[i], f1[i], fb[i]
            ph.op("dve", lambda e: e.tensor_copy(out=FB[:, 0:128].rearrange("p (h d) -> p h d", d=16), in_=fap(pp[2][:, 0:16], [(64, 8), (1, 16)])),
                  reads=[pp[2].b, PO.sub(2)], writes=[FB.sub(2)])
            ph.op("dve", lambda e: e.tensor_copy(out=FB[:, 128:256].rearrange("p (h d) -> p h d", d=16), in_=fap(pp[3][:, 0:16], [(64, 8), (1, 16)])),
                  reads=[pp[3].b, PO.sub(3)], writes=[FB.sub(3)])

        def stage_Pm(t):
            X, J, SS, SD, RS, H, HT, PO, SQ, S8, D8, R8, QN, T1, T2, U1, U2 = slot(t)
            i = t % NB
            F0, F1, FB = f0[i], f1[i], fb[i]
            ph.op("act", lambda e: e.activation(out=SQ[:, 0:640], in_=F0[:, 0:640], func=AF.Square),
                  reads=[F0.b, F1.b], writes=[SQ.b])
            ph.op("dve", lambda e: e.tensor_reduce(out=S8[:], in_=SQ[:].rearrange("p (h d) -> p h d", d=HD), axis=AX.X, op=ALU.add),
                  reads=[SQ.b], writes=[S8.b])
            rstd_ops(ph, S8, D8, R8, 1.0 / HD)
            ph.op("dve", lambda e: e.tensor_tensor(out=QN[:, 0:640].rearrange("p (h d) -> p h d", d=HD),
                                                   in0=F0[:, 0:640].rearrange("p (h d) -> p h d", d=HD),
                                                   in1=fap(R8[:, 0:10], [(1, 10), (0, HD)]), op=ALU.mult),
                  reads=[F0.b, F1.b, R8.b], writes=[QN.b])
            ph.op("dve", lambda e: e.tensor_tensor(out=QN[:, 0:512].rearrange("p (h d) -> p h d", d=HD),
                                                    in0=QN[:, 0:512].rearrange("p (h d) -> p h d", d=HD),
                                                    in1=fap(gq[:], [(0, 8), (1, HD)]), op=ALU.mult),
                  reads=[QN.b, gq.b], writes=[QN.b])
            ph.op("dve", lambda e: e.tensor_tensor(out=QN[:, 512:640].rearrange("p (h d) -> p h d", d=HD),
                                                    in0=QN[:, 512:640].rearrange("p (h d) -> p h d", d=HD),
                                                    in1=fap(gk[:], [(0, 2), (1, HD)]), op=ALU.mult),
                  reads=[QN.b, gk.b], writes=[QN.b])
            cosA = fap(ropeA[:, t, 0:64], [(0, 10), (1, 64)])
            ph.op("dve", lambda e: e.tensor_tensor(out=T1[:].rearrange("p (h d) -> p h d", d=HD),
                                                   in0=QN[:].rearrange("p (h d) -> p h d", d=HD), in1=cosA, op=ALU.mult),
                  reads=[QN.b, ropeA.b], writes=[T1.b])
            for half in range(2):
                sn = fap(ropeA[:, t, 64 + half * 16:64 + half * 16 + 16], [(0, 10), (32, 2), (1, 16)])
                dst = fap(T2[:, half * 16:half * 16 + 16], [(64, 10), (32, 2), (1, 16)])
                src = fap(QN[:, (1 - half) * 16:(1 - half) * 16 + 16], [(64, 10), (32, 2), (1, 16)])
                ph.op("dve" if half == 0 else "pool", lambda e, dst=dst, src=src, sn=sn: e.tensor_tensor(out=dst, in0=src, in1=sn, op=ALU.mult),
                      reads=[QN.b, ropeA.b], writes=[T2.b])
            ph.op("dve", lambda e: e.tensor_tensor(out=PO[:, C_QA:C_QA + 640], in0=T1[:], in1=T2[:], op=ALU.add),
                  reads=[T1.b, T2.b], writes=[PO.sub('qa')])
            for gi, c0, off in ((2, C_QB, 0), (3, C_KB, 128)):
                xin = FB[:, off:off + 128].rearrange("p (h d) -> p h d", d=16)
                cosB = fap(ropeB[:, t, 0:16], [(0, 8), (1, 16)])
                ph.op("pool", lambda e, xin=xin, cosB=cosB, off=off: e.tensor_tensor(
                    out=U1[:, off:off + 128].rearrange("p (h d) -> p h d", d=16), in0=xin, in1=cosB, op=ALU.mult),
                    reads=[FB.sub(gi), ropeB.b], writes=[U1.sub(gi)])
                for half in range(2):
                    xs = fap(FB[:, off + (1 - half) * 8:off + (1 - half) * 8 + 8], [(16, 8), (1, 8)])
                    sn = fap(ropeB[:, t, 16 + half * 8:16 + half * 8 + 8], [(0, 8), (1, 8)])
                    dst = fap(U2[:, off + half * 8:off + half * 8 + 8], [(16, 8), (1, 8)])
                    ph.op("pool", lambda e, dst=dst, xs=xs, sn=sn: e.tensor_tensor(out=dst, in0=xs, in1=sn, op=ALU.mult),
                          reads=[FB.sub(gi), ropeB.b], writes=[U2.sub(gi)])
                ph.op("dve", lambda e, c0=c0, off=off: e.tensor_tensor(
                    out=fap(PO[:, c0:c0 + 16], [(64, 8), (1, 16)]),
                    in0=U1[:, off:off + 128].rearrange("p (h d) -> p h d", d=16),
                    in1=U2[:, off:off + 128].rearrange("p (h d) -> p h d", d=16), op=ALU.add),
                    reads=[U1.sub(gi), U2.sub(gi)], writes=[PO.sub(gi)])
            ph.op("sp", lambda e: e.dma_start(out=proj[t * 128:(t + 1) * 128, :], in_=PO[:]),
                  reads=[PO.sub('qa'), PO.sub('va'), PO.sub(2), PO.sub(3), PO.sub('vb')], dma=PO.b)

        for t0 in range(min(3, nt)):
            load_x(t0)
        stage_N(0)
        stage_T(0)
        stage_M(0)
        if nt > 1:
            stage_N(1)
        for k in range(nt):
            if k + 3 < nt:
                load_x(k + 3)
            stage_Pe_a(k)
            if k + 1 < nt:
                stage_T(k + 1)
            stage_Pe_b(k)
            if k + 1 < nt:
                stage_M(k + 1)
            if k + 2 < nt:
                stage_N(k + 2)
            stage_Pm(k)
        ph.close()


def load_transposed(ph, proj, c0, tm, dstT, ptr, ident_b, row_sel=None):
    pv = proj.rearrange("(t p) c -> p t c", p=128)
    for q4 in range(4):
        ph.op("sp", lambda e, q4=q4: e.dma_start(out=tm[:, q4 * 8:(q4 + 1) * 8, :], in_=pv[:, q4 * 8:(q4 + 1) * 8, c0:c0 + 128]),
              writes=[tm.sub(q4)], dma=tm.sub(q4))
    for q4 in range(4):
        for t in range(q4 * 8, q4 * 8 + 8):
            ph.op("pe", lambda e, t=t: e.transpose(out=ptr[:, t % 8, :], in_=tm[:, t, :], identity=ident_b[:]),
                  reads=[tm.sub(q4), ident_b.b], writes=[ptr.b])
        ph.op("dve", lambda e, q4=q4: e.tensor_copy(out=dstT[:, q4 * 1024:(q4 + 1) * 1024], in_=ptr[:].rearrange("p a b -> p (a b)")),
              reads=[ptr.b], writes=[dstT.b])


def phase2(nc, l, proj, ybuf, ident_b, ident_f, wi_g, w_mi):
    ph = Phase(nc, f"p2l{l}")
    with ExitStack() as es:
        T = lambda name, shape, dt, psum=False: Tile(es, nc, f"p2l{l}_{name}", shape, dt, psum)
        kT = T("kT", [128, S], BF16)
        vext = T("vext", [128, NT * 2 * 65], BF16)
        qT = [T(f"qT{j}", [128, S], BF16) for j in range(4)]
        tm = [T(f"tm{i}", [128, NT, 128], BF16) for i in range(2)]
        pb = [T(f"pb{i}", [128, 1024], BF16) for i in range(3)]
        oT = [T(f"oT{i}", [128, 1024], F32) for i in range(2)]
        rc = [T(f"rc{i}", [128, 4], F32) for i in range(2)]
        yst = [T(f"yst{i}", [128, 4, 512], BF16) for i in range(2)]
        ptr = T("ptr", [128, 8, 128], BF16, psum=True)
        ps = [T(f"ps{i}", [128, 1024], F32, psum=True) for i in range(2)]
        pov = T("pov", [128, 1024], F32, psum=True)
        pot = T("pot", [128, 4, 128], F32, psum=True)

        v4 = vext.t[:].rearrange("p (t g c) -> p t g c", t=NT, g=2, c=65)
        ph.op("pool", lambda e: e.memset(v4[:, :, :, 64:65], 1.0), writes=[vext.sub("ones")])
        if wi_g is not None:
            wiv = w_mi[l].rearrange("(kc p) n -> kc p n", p=128)
            for k in range(8):
                ph.op("pool", lambda e, k=k: e.dma_start(out=wi_g[k][:], in_=wiv[k]), writes=[wi_g[k].b], dma=wi_g[k].b)
        pv = proj.rearrange("(t p) c -> p t c", p=128)
        for q4 in range(4):
            for g in range(2):
                ph.op("sp", lambda e, q4=q4, g=g: e.dma_start(
                    out=v4[:, q4 * 8:(q4 + 1) * 8, g, 0:64],
                    in_=pv[:, q4 * 8:(q4 + 1) * 8, C_VA + g * 64:C_VA + g * 64 + 64]),
                    writes=[vext.sub((q4, g))], dma=vext.sub((q4, g)))
        load_transposed(ph, proj, C_KA, tm[0], kT, ptr, ident_b)
        for j in range(4):
            load_transposed(ph, proj, C_QA + j * 128, tm[(j + 1) % 2], qT[j], ptr, ident_b)

        steps = [(qc, j, kt) for qc in range(S // 512) for j in range(4) for kt in range(NT)]
        N = len(steps)
        deferred = {}

        def finalize_pe(grp, qc, j):
            a = grp % 2
            Y = yst[qc % 2]
            for g in range(2):
                for c in range(4):
                    ph.op("pe", lambda e, c=c, g=g: e.transpose(out=pot[:, c, 0:65],
                                                              in_=oT[a][0:65, g * 512 + c * 128:g * 512 + (c + 1) * 128],
                                                              identity=ident_f[0:65, 0:65]),
                          reads=[oT[a].b, ident_f.b], writes=[pot.b])
                ph.op("dve", lambda e, g=g: e.reciprocal(out=rc[g][:], in_=pot[:, :, 64]), reads=[pot.b], writes=[rc[g].b])
                hc = (2 * j + g) * 64
                ph.op("dve", lambda e, g=g, hc=hc: e.tensor_tensor(out=Y[:, :, hc:hc + 64], in0=pot[:, :, 0:64],
                                                                   in1=fap(rc[g][:], [(1, 4), (0, 64)]), op=ALU.mult),
                      reads=[pot.b, rc[g].b], writes=[Y.sub(hc)])
            if j == 3:
                ph.op("sp", lambda e: e.dma_start(
                    out=ybuf[qc * 512:(qc + 1) * 512, 0:512].rearrange("(c p) f -> p c f", p=128), in_=Y[:, :, :]),
                    reads=[Y.sub(h * 64) for h in range(8)], dma=Y.b)

        for n in range(N + 4):
            if n < N:
                qc, j, kt = steps[n]
                for g in range(2):
                    r0 = g * 64
                    ph.op("pe", lambda e, n=n, j=j, r0=r0, kt=kt, qc=qc, g=g: e.matmul(
                        ps[n % 2][:, g * 512:(g + 1) * 512], lhsT=kT[r0:r0 + 64, kt * 128:(kt + 1) * 128],
                        rhs=qT[j][r0:r0 + 64, qc * 512:(qc + 1) * 512], start=True, stop=True),
                        reads=[kT.b, qT[j].b], writes=[ps[n % 2].b])
                ph.op("act", lambda e, n=n: e.activation(out=pb[n % 3][:], in_=ps[n % 2][:], func=AF.Exp, scale=SCALE),
                      reads=[ps[n % 2].b], writes=[pb[n % 3].b])
            m = n - 1
            if 0 <= m < N:
                qc, j, kt = steps[m]
                grp = m // NT
                a = grp % 2
                for g in range(2):
                    ph.op("pe", lambda e, m=m, g=g, kt=kt: e.matmul(
                        pov[0:65, g * 512:(g + 1) * 512], lhsT=v4[:, kt, g, :], rhs=pb[m % 3][:, g * 512:(g + 1) * 512],
                        start=(kt == 0), stop=(kt == NT - 1)),
                        reads=[pb[m % 3].b, vext.sub((kt // 8, g)), vext.sub("ones")], writes=[pov.b])
                if kt == NT - 1:
                    ph.op("dve", lambda e, a=a: e.tensor_copy(out=oT[a][0:65, :], in_=pov[0:65, :]),
                          reads=[pov.b], writes=[oT[a].b])
                    deferred.setdefault(n + 3, []).append((grp, qc, j))
            for args in deferred.pop(n, []):
                finalize_pe(*args)
        assert not deferred
        ph.close()


def dram_ap(base, row0, c0, dims):
    return bass.AP(base.tensor, base.offset + row0 * IN_W + c0, [list(d) for d in dims])


def pi_dma_specs(d):
    nb = NT // d
    specs = []
    if d == 1:
        for q4 in range(4):
            specs.append(((q4 * 8, 8, 1), q4 * 8 * 128, (1, 128)))
    elif d == 4:
        for r in range(4):
            specs.append(((r * nb, nb, 1), r, (d, 128 * d)))
    else:
        for ib in range(nb):
            specs.append(((ib, d, nb), d * 128 * ib, (d, 1)))
    return specs


def phase3(nc, l, proj, ybuf, c_mask, ident_b, ident_f):
    ph = Phase(nc, f"p3l{l}")
    with ExitStack() as es:
        T = lambda name, shape, dt, psum=False: Tile(es, nc, f"p3l{l}_{name}", shape, dt, psum)
        mask_f = T("mask_f", [128, 384], F32)
        mask = T("mask", [128, 384], BF16)
        qtm = [T(f"qtm{i}", [128, NT, 128], BF16) for i in range(2)]
        ktm = [T(f"ktm{i}", [128, NT, 128], BF16) for i in range(2)]
        qT = [T(f"qT{i}", [128, S], BF16) for i in range(2)]
        kT = [T(f"kT{i}", [128, S], BF16) for i in range(2)]
        vext = [T(f"vext{i}", [128, NT * 2 * 65], BF16) for i in range(2)]
        accT = [T(f"acc{i}", [128, S], F32) for i in range(2)]
        pb = [T(f"pb{i}", [128, 384], BF16) for i in range(3)]
        pbm = [T(f"pbm{i}", [128, 384], BF16) for i in range(3)]
        rc = [T(f"rc{i}", [128, 4], F32) for i in range(2)]
        yst = [T(f"yst{i}", [128, 4, 128], BF16) for i in range(2)]
        ptr = T("ptr", [128, 8, 128], BF16, psum=True)
        ptr2 = T("ptr2", [128, 8, 128], BF16, psum=True)
        potc = [T(f"potc{i}", [128, 4, 65], F32) for i in range(2)]
        ps = [T(f"ps{i}", [128, 512], F32, psum=True) for i in range(3)]
        pov = [T(f"pov{i}", [128, 512], F32, psum=True) for i in range(2)]
        pot = T("pot", [128, 4, 128], F32, psum=True)

        ph.op("sp", lambda e: e.dma_start(out=mask_f[:], in_=c_mask[:, :]), writes=[mask_f.b], dma=mask_f.b)
        ph.op("dve", lambda e: e.tensor_copy(out=mask[:], in_=mask_f[:]), reads=[mask_f.b], writes=[mask.b])
        v4 = [v.t[:].rearrange("p (t g c) -> p t g c", t=NT, g=2, c=65) for v in vext]
        for i in range(2):
            ph.op("pool", lambda e, i=i: e.memset(v4[i][:, :, :, 64:65], 1.0), writes=[vext[i].sub("ones")])

        units = [(j, d) for j in range(NPAIRS) for d in PATTERNS]

        def issue_loads(u):
            j, d = units[u]
            i = u % 2
            for k, (ts, row0, (rs1, rs2)) in enumerate(pi_dma_specs(d)):
                t0, cnt, tstep = ts
                tsl = slice(t0, t0 + (cnt - 1) * tstep + 1, tstep)
                for (tmt, c0) in ((qtm[i], C_QB + j * 128), (ktm[i], C_KB + j * 128)):
                    ph.op("sp", lambda e, tmt=tmt, c0=c0, tsl=tsl, row0=row0, rs1=rs1, rs2=rs2, cnt=cnt: e.dma_start(
                        out=tmt[:, tsl, :], in_=dram_ap(proj, row0, c0, [(rs1 * IN_W, 128), (rs2 * IN_W, cnt), (1, 128)])),
                        writes=[tmt.sub(k)], dma=tmt.sub(k), after=[tmt.b])
                for g in range(2):
                    c0 = C_VB + j * 128 + g * 64
                    ph.op("sp", lambda e, i=i, g=g, c0=c0, tsl=tsl, row0=row0, rs1=rs1, rs2=rs2, cnt=cnt: e.dma_start(
                        out=v4[i][:, tsl, g, 0:64], in_=dram_ap(proj, row0, c0, [(rs1 * IN_W, 128), (rs2 * IN_W, cnt), (1, 64)])),
                        writes=[vext[i].sub((k, g))], dma=vext[i].sub((k, g)), after=[vext[i].b])
            return len(pi_dma_specs(d))

        def tile_sub(d, t):
            nb = NT // d
            if d == 1:
                return t // 8
            if d == 4:
                return t // nb
            return t % nb

        def do_transposes(u):
            j, d = units[u]
            i = u % 2
            n = 0
            for (tmt, dst) in ((qtm[i], qT[i]), (ktm[i], kT[i])):
                for q4 in range(4):
                    P, eng = (ptr, "dve") if n % 2 == 0 else (ptr2, "act")
                    n += 1
                    for t in range(q4 * 8, q4 * 8 + 8):
                        ph.op("pe", lambda e, t=t, tmt=tmt, P=P: e.transpose(out=P[:, t % 8, :], in_=tmt[:, t, :], identity=ident_b[:]),
                              reads=[tmt.sub(tile_sub(d, t)), tmt.b, ident_b.b], writes=[P.b])
                    if eng == "dve":
                        ph.op("dve", lambda e, q4=q4, dst=dst, P=P: e.tensor_copy(out=dst[:, q4 * 1024:(q4 + 1) * 1024],
                                                                               in_=P[:].rearrange("p a b -> p (a b)")),
                              reads=[P.b], writes=[dst.sub(q4)])
                    else:
                        ph.op("act", lambda e, q4=q4, dst=dst, P=P: e.copy(out=dst[:, q4 * 1024:(q4 + 1) * 1024],
                                                                        in_=P[:].rearrange("p a b -> p (a b)")),
                              reads=[P.b], writes=[dst.sub(q4)])

        def finalize_pair(j):
            for qc in range(S // 512):
                Y = yst[qc % 2]
                for g in range(2):
                    for c in range(4):
                        ph.op("pe", lambda e, c=c, g=g, qc=qc: e.transpose(
                            out=pot[:, c, 0:65], in_=accT[g][0:65, qc * 512 + c * 128:qc * 512 + (c + 1) * 128],
                            identity=ident_f[0:65, 0:65]), reads=[accT[g].b, ident_f.b], writes=[pot.b])
                    a = g
                    PC = potc[g]
                    ph.op("dve", lambda e, PC=PC: e.tensor_copy(out=PC[:, :, :], in_=pot[:, :, 0:65]), reads=[pot.b], writes=[PC.b])
                    ph.op("dve", lambda e, a=a, PC=PC: e.reciprocal(out=rc[a][:], in_=PC[:, :, 64]), reads=[PC.b], writes=[rc[a].b])
                    ph.op("pool", lambda e, a=a, Y=Y, g=g, PC=PC: e.tensor_tensor(out=Y[:, :, g * 64:(g + 1) * 64], in0=PC[:, :, 0:64],
                                                                                 in1=fap(rc[a][:], [(1, 4), (0, 64)]), op=ALU.mult),
                          reads=[PC.b, rc[a].b], writes=[Y.sub(g)])
                ph.op("sp", lambda e, Y=Y, qc=qc, j=j: e.dma_start(
                    out=ybuf[qc * 512:(qc + 1) * 512, 512 + j * 128:512 + (j + 1) * 128].rearrange("(c p) f -> p c f", p=128),
                    in_=Y[:, :, :]), reads=[Y.sub(0), Y.sub(1)], dma=Y.b)

        issue_loads(0)
        cnt = 0
        for u, (j, d) in enumerate(units):
            i = u % 2
            nb = NT // d
            if u + 1 < len(units):
                issue_loads(u + 1)
            do_transposes(u)
            steps = [(g, tq) for g in range(2) for tq in range(NT)]
            N = len(steps)
            info = {}
            LAG = 2
            for n in range(N + LAG):
                if n < N:
                    g, tq = steps[n]
                    r0 = g * 64
                    ib = tq % nb
                    tks = [(tq + o, o + 1) for o in (-1, 0, 1) if 0 <= ib + o < nb]
                    k = cnt % 3
                    cnt += 1
                    for c, (tk, mi) in enumerate(tks):
                        ph.op("pe", lambda e, k=k, c=c, tk=tk, tq=tq, r0=r0, i=i: e.matmul(
                            ps[k][:, c * 128:(c + 1) * 128], lhsT=kT[i][r0:r0 + 64, tk * 128:(tk + 1) * 128],
                            rhs=qT[i][r0:r0 + 64, tq * 128:(tq + 1) * 128], start=True, stop=True),
                            reads=[kT[i].sub(q) for q in range(4)] + [qT[i].sub(q) for q in range(4)], writes=[ps[k].b])
                    w = len(tks) * 128
                    m0 = tks[0][1] * 128
                    ph.op("act", lambda e, k=k, w=w: e.activation(out=pb[k][:, 0:w], in_=ps[k][:, 0:w], func=AF.Exp, scale=SCALE),
                          reads=[ps[k].b], writes=[pb[k].b])
                    ph.op("pool" if cnt % 3 == 0 else "dve", lambda e, k=k, w=w, m0=m0: e.tensor_tensor(out=pbm[k][:, 0:w], in0=pb[k][:, 0:w],
                                                                             in1=mask[:, m0:m0 + w], op=ALU.mult),
                          reads=[pb[k].b, mask.b], writes=[pbm[k].b])
                    info[n] = (k, tks)
                m = n - LAG
                if 0 <= m < N:
                    g, tq = steps[m]
                    k, tks = info[m]
                    a = m % 2
                    for c, (tk, mi) in enumerate(tks):
                        ph.op("pe", lambda e, a=a, c=c, tk=tk, g=g, k=k, i=i, nk=len(tks): e.matmul(
                            pov[a][0:65, 0:128], lhsT=v4[i][:, tk, g, :], rhs=pbm[k][:, c * 128:(c + 1) * 128],
                            start=(c == 0), stop=(c == nk - 1)),
                            reads=[pbm[k].b, vext[i].sub((tile_sub(d, tk), g)), vext[i].sub("ones"), vext[i].b], writes=[pov[a].b])
                    r, ib = tq // nb, tq % nb
                    col0 = r + d * 128 * ib
                    av = fap(accT[g][0:65, col0:col0 + 1], [(d, 128)])
                    if d == PATTERNS[0]:
                        ph.op("dve", lambda e, av=av, a=a: e.tensor_copy(out=av, in_=pov[a][0:65, 0:128]),
                              reads=[pov[a].b], writes=[accT[g].b])
                    else:
                        ph.op("dve", lambda e, av=av, a=a: e.tensor_tensor(out=av, in0=av, in1=pov[a][0:65, 0:128], op=ALU.add),
                              reads=[pov[a].b, accT[g].b], writes=[accT[g].b])
            if d == PATTERNS[-1]:
                finalize_pair(j)
        ph.close()


def phase4(nc, l, x_src, ybuf, w_out, w_mi, w_mo, g_oa, g_ob, g_n2, g_fin, dst, last, ident_b, nt, wi_pre=None):
    ph = Phase(nc, f"p4l{l}")
    with ExitStack() as es:
        T = lambda name, shape, dt, psum=False: Tile(es, nc, f"p4l{l}_{name}", shape, dt, psum)
        wo = [T(f"wo{k}", [128, D], BF16) for k in range(8)]
        wi = wi_pre if wi_pre is not None else [T(f"wi{k}", [128, DFF], BF16) for k in range(8)]
        wo2 = [T(f"wo2{k}", [128, 4, D], BF16) for k in range(8)]
        goa = T("goa", [128, 512], F32)
        gob = T("gob", [128, 512], F32)
        g2 = T("g2", [128, D], F32)
        gfin = T("gfin", [128, D], F32) if last else None
        NB = 2
        y_t = [T(f"y{i}", [128, D], BF16) for i in range(3)]
        x_t = [T(f"x{i}", [128, D], F32) for i in range(3)]
        yn = [T(f"yn{i}", [128, D], BF16) for i in range(NB)]
        yT = [T(f"yT{i}", [128, 8, 128], BF16) for i in range(NB)]
        h2 = [T(f"h2{i}", [128, D], BF16) for i in range(NB)]
        h2T = [T(f"h2T{i}", [128, 8, 128], BF16) for i in range(NB)]
        st = [{k: T(f"{k}{i}", [128, 2], F32) for k in ("ssy", "sdy", "rsy", "ss2", "sd2", "rs2", "ssf", "sdf", "rsf")} for i in range(NB)]
        junk = T("junk", [128, D], BF16)
        r_sb = [T(f"r{i}", [128, 512], F32) for i in range(2)]
        u_bf = [T(f"u{i}", [128, 512], BF16) for i in range(2)]
        uT = T("uT", [128, 32, 128], BF16)
        pT = [T(f"pT{i}", [128, 8, 128], BF16, psum=True) for i in range(2)]
        pw = [T(f"pw{i}", [128, 512], F32, psum=True) for i in range(2)]
        pu = [T(f"pu{i}", [128, 512], F32, psum=True) for i in range(2)]
        po2 = [T(f"po2{i}", [128, 512], F32, psum=True) for i in range(2)]

        wov = w_out[l].rearrange("(kc p) n -> kc p n", p=128)
        wiv = w_mi[l].rearrange("(kc p) n -> kc p n", p=128)
        wo2v = w_mo[l].rearrange("(k4 kc p) n -> k4 p kc n", p=128, kc=4)
        for k in range(8):
            ph.op("pool", lambda e, k=k: e.dma_start(out=wo[k][:], in_=wov[k]), writes=[wo[k].b], dma=wo[k].b)
        if wi_pre is None:
            for k in range(8):
                ph.op("pool", lambda e, k=k: e.dma_start(out=wi[k][:], in_=wiv[k]), writes=[wi[k].b], dma=wi[k].b)
        for k in range(8):
            ph.op("pool", lambda e, k=k: e.dma_start(out=wo2[k][:], in_=wo2v[k]), writes=[wo2[k].b], dma=wo2[k].b)
        ph.op("sp", lambda e: e.dma_start(out=goa[:], in_=g_oa[l:l + 1, :].partition_broadcast(128)), writes=[goa.b], dma=goa.b)
        ph.op("sp", lambda e: e.dma_start(out=gob[:], in_=g_ob[l:l + 1, :].partition_broadcast(128)), writes=[gob.b], dma=gob.b)
        ph.op("sp", lambda e: e.dma_start(out=g2[:], in_=g_n2[l:l + 1, :].partition_broadcast(128)), writes=[g2.b], dma=g2.b)
        if last:
            ph.op("sp", lambda e: e.dma_start(out=gfin[:], in_=g_fin[0:1, :].partition_broadcast(128)), writes=[gfin.b], dma=gfin.b)

        def norm_scale(src, ss, sd, rs, dst_t, gain, width, col0, ncols):
            for j in range(ncols):
                c0 = col0 + j * width
                ph.op("act", lambda e, c0=c0, j=j: e.activation(out=junk[:, c0:c0 + width], in_=src[:, c0:c0 + width],
                                                              func=AF.Square, accum_out=ss[:, j:j + 1]),
                      reads=[src.b], writes=[ss.b])
            rstd_ops(ph, ss, sd, rs, 1.0 / width, ncols)
            for j in range(ncols):
                c0 = col0 + j * width
                ph.op("dve", lambda e, c0=c0, j=j: e.scalar_tensor_tensor(
                    out=dst_t[:, c0:c0 + width], in0=src[:, c0:c0 + width], scalar=rs[:, j:j + 1], in1=gain[j][:, 0:width],
                    op0=ALU.mult, op1=ALU.mult), reads=[src.b, rs.b, gain[j].b], writes=[dst_t.b])

        def A0_load(t):
            ph.op("sp", lambda e: e.dma_start(out=y_t[t % 3][:], in_=ybuf[t * 128:(t + 1) * 128, :]), writes=[y_t[t % 3].b], dma=y_t[t % 3].b)
            ph.op("sp", lambda e: e.dma_start(out=x_t[t % 3][:], in_=x_src[t * 128:(t + 1) * 128, :]), writes=[x_t[t % 3].b], dma=x_t[t % 3].b)

        def A0_norm(t):
            i = t % NB
            norm_scale(y_t[t % 3], st[i]["ssy"], st[i]["sdy"], st[i]["rsy"], yn[i], [goa, gob], 512, 0, 2)

        def A1t(t, bank):
            i = t % NB
            transpose8(ph, yn[i], yT[i], pT[bank], ident_b, "act")

        def A1m(t):
            i = t % NB
            for nh in range(2):
                for kc in range(8):
                    ph.op("pe", lambda e, nh=nh, kc=kc: e.matmul(pw[nh][:, :], lhsT=yT[i][:, kc, :], rhs=wo[kc][:, nh * 512:(nh + 1) * 512],
                                                               start=(kc == 0), stop=(kc == 7)),
                          reads=[yT[i].b, wo[kc].b], writes=[pw[nh].b])
            for nh in range(2):
                ph.op("dve", lambda e, nh=nh: e.tensor_tensor(out=x_t[t % 3][:, nh * 512:(nh + 1) * 512], in0=x_t[t % 3][:, nh * 512:(nh + 1) * 512],
                                                             in1=pw[nh][:, :], op=ALU.add),
                      reads=[x_t[t % 3].b, pw[nh].b], writes=[x_t[t % 3].b])
            if DEBUG_XM is not None:
                ph.op("sp", lambda e: e.dma_start(out=DEBUG_XM[t * 128:(t + 1) * 128, :], in_=x_t[t % 3][:]), reads=[x_t[t % 3].b], dma=x_t[t % 3].sub("dbg"))
            norm_scale(x_t[t % 3], st[i]["ss2"], st[i]["sd2"], st[i]["rs2"], h2[i], [g2], D, 0, 1)

        def A2(t):
            i = t % NB
            transpose8(ph, h2[i], h2T[i], pT[1], ident_b, "act")

        def u_transposes(c):
            x = (c // 2) % 2
            for q in range(4):
                ph.op("pe", lambda e, q=q: e.transpose(out=pT[x][:, (c % 2) * 4 + q, :], in_=u_bf[c % 2][:, q * 128:(q + 1) * 128],
                                                      identity=ident_b[:]),
                      reads=[u_bf[c % 2].b, ident_b.b], writes=[pT[x].b])
            if c % 2 == 1:
                ph.op("dve", lambda e: e.tensor_copy(out=uT[:, (c - 1) * 4:(c + 1) * 4, :], in_=pT[x][:]),
                      reads=[pT[x].b], writes=[uT.sub(c // 2)])

        def B(t, upto=8, hook=None):
            i = t % NB
            for c in range(upto):
                for kc in range(8):
                    ph.op("pe", lambda e, c=c, kc=kc: e.matmul(pu[c % 2][:, :], lhsT=h2T[i][:, kc, :], rhs=wi[kc][:, c * 512:(c + 1) * 512],
                                                             start=(kc == 0), stop=(kc == 7)),
                          reads=[h2T[i].b, wi[kc].b], writes=[pu[c % 2].b])
                ph.op("act", lambda e, c=c: e.activation(out=r_sb[c % 2][:], in_=pu[c % 2][:], func=AF.Relu),
                      reads=[pu[c % 2].b], writes=[r_sb[c % 2].b])
                ph.op("dve", lambda e, c=c: e.tensor_tensor(out=u_bf[c % 2][:], in0=r_sb[c % 2][:], in1=r_sb[c % 2][:], op=ALU.mult),
                      reads=[r_sb[c % 2].b], writes=[u_bf[c % 2].b])
                if c >= 1:
                    u_transposes(c - 1)
                if c == 6 and hook is not None:
                    hook()

        def C_mm(t):
            i = t % NB
            for nh in range(2):
                for kc in range(32):
                    ph.op("pe", lambda e, nh=nh, kc=kc: e.matmul(po2[nh][:, :], lhsT=uT[:, kc, :],
                                                               rhs=wo2[kc // 4][:, kc % 4, nh * 512:(nh + 1) * 512],
                                                               start=(kc == 0), stop=(kc == 31)),
                          reads=[uT.sub(kc // 8), wo2[kc // 4].b], writes=[po2[nh].b])
            for nh in range(2):
                ph.op("dve", lambda e, nh=nh: e.tensor_tensor(out=x_t[t % 3][:, nh * 512:(nh + 1) * 512], in0=x_t[t % 3][:, nh * 512:(nh + 1) * 512],
                                                             in1=po2[nh][:, :], op=ALU.add),
                      reads=[x_t[t % 3].b, po2[nh].b], writes=[x_t[t % 3].b])

        def C_fin(t):
            i = t % NB
            if last:
                norm_scale(x_t[t % 3], st[i]["ssf"], st[i]["sdf"], st[i]["rsf"], x_t[t % 3], [gfin], D, 0, 1)
            ph.op("sp", lambda e: e.dma_start(out=dst[t * 128:(t + 1) * 128, :], in_=x_t[t % 3][:]), reads=[x_t[t % 3].b], dma=x_t[t % 3].b)

        if SIMPLE_P4:
            for t in range(nt):
                A0_load(t)
                A0_norm(t)
                A1t(t, 0)
                A1m(t)
                A2(t)
                B(t)
                u_transposes(7)
                C_mm(t)
                C_fin(t)
        else:
            A0_load(0)
            if nt > 1:
                A0_load(1)
            A0_norm(0)
            A1t(0, 0)
            A1m(0)
            A2(0)
            for t in range(nt):
                if t + 2 < nt:
                    A0_load(t + 2)
                if t + 1 < nt:
                    A0_norm(t + 1)
                B(t, hook=(lambda t=t: A1t(t + 1, 1)) if t + 1 < nt else None)
                if t + 1 < nt:
                    A1m(t + 1)
                u_transposes(7)
                C_mm(t)
                if t + 1 < nt:
                    A2(t + 1)
                C_fin(t)
        ph.close()


def _perm_in():
    qa = []
    for j in range(4):
        for h in (j, j + 4):
            qa += list(range(h * 64, h * 64 + 64))
    return np.array(qa + list(range(512, IN_W)), dtype=np.int64)


def _consts():
    ident = np.eye(128, dtype=np.float32)
    t = np.arange(S)
    half = 16
    fr = (10000.0 ** (-np.arange(half, dtype=np.float32) / half)).astype(np.float32)
    ang_r = (t // 64).astype(np.float32)[:, None] * fr[None, :]
    ang_c = (t % 64).astype(np.float32)[:, None] * fr[None, :]
    cr, sr, cc, sc = np.cos(ang_r), np.sin(ang_r), np.cos(ang_c), np.sin(ang_c)
    ropeA = np.concatenate([cr, cr, cc, cc, -sr, sr, -sc, sc], axis=1).astype(np.float32)
    fb = (500000.0 ** (-np.arange(8, dtype=np.float32) / 8)).astype(np.float32)
    ang = t.astype(np.float32)[:, None] * fb[None, :]
    cb, sb = np.cos(ang), np.sin(ang)
    ropeB = np.concatenate([cb, cb, -sb, sb], axis=1).astype(np.float32)
    kk = np.arange(128)[:, None]
    qq = np.arange(128)[None, :]
    mask = np.concatenate([(np.abs(o + kk - qq) <= 64).astype(np.float32) for o in (-128, 0, 128)], axis=1)
    return ident, ropeA, ropeB, mask


def make_in_maps(inputs, n_cores=8):
    perm = _perm_in()
    ident, ropeA, ropeB, mask = _consts()
    rowperm = np.concatenate([perm[:512], np.arange(512, 1024)])
    w_in = np.ascontiguousarray(np.asarray(inputs["w_in"], dtype=np.float32)[:, :, perm])
    shared = {
        "w_in": w_in,
        "w_out": np.ascontiguousarray(np.asarray(inputs["w_out"], dtype=np.float32)[:, rowperm, :]),
        "w_mlp_in": np.ascontiguousarray(inputs["w_mlp_in"], dtype=np.float32),
        "w_mlp_out": np.ascontiguousarray(inputs["w_mlp_out"], dtype=np.float32),
        "norm1": np.ascontiguousarray(inputs["norm1"], dtype=np.float32),
        "norm2": np.ascontiguousarray(inputs["norm2"], dtype=np.float32),
        "q_norm": np.ascontiguousarray(inputs["q_norm"], dtype=np.float32),
        "k_norm": np.ascontiguousarray(inputs["k_norm"], dtype=np.float32),
        "out_norm_a": np.ascontiguousarray(np.asarray(inputs["out_norm_a"], dtype=np.float32)[:, perm[:512]]),
        "out_norm_b": np.ascontiguousarray(inputs["out_norm_b"], dtype=np.float32),
        "final_norm": np.ascontiguousarray(np.asarray(inputs["final_norm"], dtype=np.float32).reshape(1, D)),
        "c_ident": ident, "c_ropeA": ropeA, "c_ropeB": ropeB, "c_mask": mask,
    }
    x = np.asarray(inputs["x"], dtype=np.float32)
    maps = []
    for c in range(n_cores):
        m = dict(shared)
        m["x"] = np.ascontiguousarray(x[(c // 2) % x.shape[0]])
        maps.append(m)
    return maps


def kernel(**inputs):
    nc = build_program()
    in_maps = make_in_maps(inputs, 8)
    res = run_bass_kernel_spmd(nc, in_maps, core_ids=list(range(8)))
    outs = [np.asarray(res.results[2 * b]["out"], dtype=np.float32) for b in range(4)]
    return np.stack(outs, axis=0)
```
